# Optimizing a Trainium2 kernel written in Bass

```python
import math
import functools
import jax
import jax.numpy as jnp
from jax import lax
import numpy as np

D_MODEL = 1024
BATCH = 16
SEQ = 2048
DEPTH = 2
DEC_BATCH = 128
DEC_SEQ = 1
PAST_LEN = 8192
PAGE_SIZE = 128

N_META = 16
N_A_LAYERS = DEPTH // 2
N_B_LAYERS = DEPTH - N_A_LAYERS
NORM_EPS = 1e-6

GDN_HEADS = 8
GDN_DK = 128
GDN_DV = 128
GDN_CONV = 4
GDN_CHUNK = 64
GDN_KEY = GDN_HEADS * GDN_DK
GDN_VAL = GDN_HEADS * GDN_DV
GDN_QKV = 2 * GDN_KEY + GDN_VAL
GDN_IN = GDN_QKV + GDN_VAL + 2 * GDN_HEADS

MLA_HEADS = 8
MLA_Q_LORA = 384
MLA_KV_LORA = 256
MLA_NOPE = 128
MLA_ROPE = 64
MLA_V = 128
MLA_SCALE = 1.0 / math.sqrt(MLA_NOPE + MLA_ROPE)
ROPE_THETA = 10000.0
Q_BLOCK = 128

D_FF = 2816
FFN_CONV = 3

kernel_name = 'yoco_gdn_mla_convffn_step'


def rmsnorm(x, w):
    xf = x.astype(jnp.float32)
    y = xf * lax.rsqrt(jnp.mean(xf * xf, axis=-1, keepdims=True) + NORM_EPS)
    return (y * w.astype(jnp.float32)).astype(x.dtype)


def l2norm(x):
    xf = x.astype(jnp.float32)
    return xf * lax.rsqrt(jnp.sum(xf * xf, axis=-1, keepdims=True) + NORM_EPS)


def rope(x, positions):
    half = x.shape[-1] // 2
    inv = ROPE_THETA ** (-jnp.arange(half, dtype=jnp.float32) / half)
    ang = positions.astype(jnp.float32)[:, None] * inv[None, :]
    cos = jnp.cos(ang)[None, :, None, :]
    sin = jnp.sin(ang)[None, :, None, :]
    x1 = x[..., :half].astype(jnp.float32)
    x2 = x[..., half:].astype(jnp.float32)
    return jnp.concatenate([x1 * cos - x2 * sin, x2 * cos + x1 * sin], axis=-1).astype(x.dtype)


def causal_dwconv(x, prev, w):
    width = w.shape[0]
    L = x.shape[1]
    xp = jnp.concatenate([prev.astype(x.dtype), x], axis=1)
    out = sum(xp[:, j:j + L] * w[j] for j in range(width))
    return out, xp[:, L:]


def gated_delta_chunked(q, k, v, g, beta, S0, chunk):
    f32 = jnp.float32
    B, L, H, DK = q.shape
    DV = v.shape[-1]
    n = L // chunk

    def blocks(t):
        t = t.astype(f32).reshape((B, n, chunk, H) + t.shape[3:])
        return jnp.moveaxis(t, (1, 3), (0, 2))

    qc = blocks(q) * (DK ** -0.5)
    kc = blocks(k)
    vc = blocks(v)
    bc = blocks(beta)
    gc = jnp.cumsum(blocks(g), axis=-1)
    idx = jnp.arange(chunk)
    causal = idx[:, None] >= idx[None, :]
    decay = jnp.exp(jnp.where(causal, gc[..., :, None] - gc[..., None, :], -jnp.inf))
    kb = kc * bc[..., None]
    strict = jnp.where(idx[:, None] > idx[None, :],
                       jnp.einsum('nbhcd,nbhed->nbhce', kb, kc) * decay, 0.0)
    rhs = jnp.concatenate([vc * bc[..., None], kb * jnp.exp(gc)[..., None]], axis=-1)
    uw = lax.linalg.triangular_solve(strict, rhs, left_side=True, lower=True, unit_diagonal=True)
    u, w = uw[..., :DV], uw[..., DV:]
    attn = jnp.einsum('nbhcd,nbhed->nbhce', qc, kc) * decay
    g_last = gc[..., -1]
    q_dec = qc * jnp.exp(gc)[..., None]
    k_tail = kc * jnp.exp(g_last[..., None] - gc)[..., None]

    def step(S, xs):
        attn_n, u_n, w_n, q_n, k_n, gl_n = xs
        v_new = u_n - jnp.einsum('bhcd,bhdv->bhcv', w_n, S)
        o_n = jnp.einsum('bhcd,bhdv->bhcv', q_n, S) + jnp.einsum('bhce,bhev->bhcv', attn_n, v_new)
        S = S * jnp.exp(gl_n)[..., None, None] + jnp.einsum('bhcd,bhcv->bhdv', k_n, v_new)
        return S, o_n

    S, o = lax.scan(step, S0.astype(f32), (attn, u, w, q_dec, k_tail, g_last))
    return jnp.moveaxis(o, (0, 2), (1, 3)).reshape(B, L, H, DV), S


def gated_delta_mixer(h, conv_prev, S0, segments, w_in, conv_w, a_log, dt_bias, out_norm, w_out):
    B, L, _ = h.shape
    proj = h @ w_in
    qkv, z, a, b = jnp.split(proj, [GDN_QKV, GDN_QKV + GDN_VAL, GDN_QKV + GDN_VAL + GDN_HEADS], axis=-1)
    qkv, conv_new = causal_dwconv(qkv, conv_prev, conv_w)
    qkv = jax.nn.silu(qkv)
    q, k, v = jnp.split(qkv, [GDN_KEY, 2 * GDN_KEY], axis=-1)
    q = l2norm(q.reshape(B, L, GDN_HEADS, GDN_DK))
    k = l2norm(k.reshape(B, L, GDN_HEADS, GDN_DK))
    v = v.reshape(B, L, GDN_HEADS, GDN_DV)
    beta = jax.nn.sigmoid(b.astype(jnp.float32))
    g = -jnp.exp(a_log.astype(jnp.float32)) * jax.nn.softplus(a.astype(jnp.float32) + dt_bias.astype(jnp.float32))
    S = S0
    outs = []
    for start, stop, chunk in segments:
        o_seg, S = gated_delta_chunked(q[:, start:stop], k[:, start:stop], v[:, start:stop],
                                       g[:, start:stop], beta[:, start:stop], S, chunk)
        outs.append(o_seg)
    o = jnp.concatenate(outs, axis=1)
    o = rmsnorm(o, out_norm) * jax.nn.silu(z.reshape(B, L, GDN_HEADS, GDN_DV).astype(jnp.float32))
    return o.reshape(B, L, GDN_VAL).astype(h.dtype) @ w_out, conv_new, S.astype(S0.dtype)


def mla_shared_kv(x, positions, kv_norm, kv_w_a, kv_a_norm):
    ckv = rmsnorm(x, kv_norm) @ kv_w_a
    c = rmsnorm(ckv[..., :MLA_KV_LORA], kv_a_norm)
    k_rope = rope(ckv[..., None, MLA_KV_LORA:], positions)[..., 0, :]
    return c, k_rope


def latent_scores(q_lat, q_pe, c, k_rope):
    s = jnp.einsum('bqhr,bkr->bhqk', q_lat, c) + jnp.einsum('bqhd,bkd->bhqk', q_pe, k_rope)
    return s.astype(jnp.float32) * MLA_SCALE


def masked_latent_attention(q_lat, q_pe, q_pos, c, k_rope, k_pos):
    s = latent_scores(q_lat, q_pe, c, k_rope)
    s = jnp.where(k_pos[None, None, None, :] <= q_pos[None, None, :, None], s, -jnp.inf)
    prob = jax.nn.softmax(s, axis=-1).astype(c.dtype)
    return jnp.einsum('bhqk,bkr->bqhr', prob, c)


def prompt_attend(q_lat, q_pe, c, k_rope):
    B, L, H, R = q_lat.shape
    pos = jnp.arange(L, dtype=jnp.int32)
    o_meta = masked_latent_attention(q_lat[:, :N_META], q_pe[:, :N_META], pos[:N_META],
                                     c[:, :N_META], k_rope[:, :N_META], pos[:N_META])
    nb = (L - N_META) // Q_BLOCK

    def blocks(t):
        return jnp.moveaxis(t[:, N_META:].reshape((B, nb, Q_BLOCK) + t.shape[2:]), 1, 0)

    o_real = lax.map(lambda a: masked_latent_attention(a[0], a[1], a[2], c, k_rope, pos),
                     (blocks(q_lat), blocks(q_pe), pos[N_META:].reshape(nb, Q_BLOCK)))
    o_real = jnp.moveaxis(o_real, 0, 1).reshape(B, L - N_META, H, R)
    return jnp.concatenate([o_meta, o_real], axis=1)


def sample_attend(q_lat, q_pe, c, k_rope, c_past, k_past):
    T = q_lat.shape[1]
    P = c_past.shape[1]
    s_past = latent_scores(q_lat, q_pe, c_past, k_past)
    idx = jnp.arange(T)
    s_new = jnp.where(idx[None, None, None, :] <= idx[None, None, :, None],
                      latent_scores(q_lat, q_pe, c, k_rope), -jnp.inf)
    prob = jax.nn.softmax(jnp.concatenate([s_past, s_new], axis=-1), axis=-1).astype(c.dtype)
    return (jnp.einsum('bhqk,bkr->bqhr', prob[..., :P], c_past)
            + jnp.einsum('bhqk,bkr->bqhr', prob[..., P:], c))


def mla_mixer(h, positions, c, k_rope, attend, w_q_a, q_a_norm, w_q_b, w_uk, w_uv, w_out):
    B, L, _ = h.shape
    q = (rmsnorm(h @ w_q_a, q_a_norm) @ w_q_b).reshape(B, L, MLA_HEADS, MLA_NOPE + MLA_ROPE)
    q_pe = rope(q[..., MLA_NOPE:], positions)
    q_lat = jnp.einsum('blhn,rhn->blhr', q[..., :MLA_NOPE], w_uk)
    o_lat = attend(q_lat, q_pe, c, k_rope)
    o = jnp.einsum('blhr,rhv->blhv', o_lat, w_uv).reshape(B, L, MLA_HEADS * MLA_V)
    return o @ w_out


def conv_ffn(h, prev, w_up, conv_w, conv_b, w_down):
    u, new_prev = causal_dwconv(h @ w_up, prev, conv_w)
    gate, val = jnp.split(u + conv_b, 2, axis=-1)
    return (jax.nn.silu(gate) * val) @ w_down, new_prev


def trunk(x, positions, segments, delta_S0, delta_conv0, ffn_conv0, attend, p):
    new_S, new_dconv, new_fconv = [], [], []
    c_kv, k_rope = None, None
    for layer in range(DEPTH):
        if layer < N_A_LAYERS:
            i = layer
            o, dconv, S = gated_delta_mixer(rmsnorm(x, p['a_norm_pre'][i]), delta_conv0[i], delta_S0[i], segments,
                                            p['a_w_in'][i], p['a_conv_w'][i], p['a_log'][i], p['a_dt_bias'][i],
                                            p['a_out_norm'][i], p['a_w_out'][i])
            x = x + rmsnorm(o, p['a_norm_post'][i])
            new_S.append(S)
            new_dconv.append(dconv)
        else:
            j = layer - N_A_LAYERS
            if j == 0:
                c_kv, k_rope = mla_shared_kv(x, positions, p['kv_norm'], p['kv_w_a'], p['kv_a_norm'])
            o = mla_mixer(rmsnorm(x, p['b_norm_pre'][j]), positions, c_kv, k_rope, attend,
                          p['b_w_q_a'][j], p['b_q_a_norm'][j], p['b_w_q_b'][j],
                          p['kv_w_uk'], p['kv_w_uv'], p['b_w_out'][j])
            x = x + rmsnorm(o, p['b_norm_post'][j])
        o, fconv = conv_ffn(rmsnorm(x, p['f_norm_pre'][layer]), ffn_conv0[layer],
                            p['f_w_up'][layer], p['f_conv_w'][layer], p['f_conv_b'][layer], p['f_w_down'][layer])
        x = x + rmsnorm(o, p['f_norm_post'][layer])
        new_fconv.append(fconv)
    return x, jnp.stack(new_S), jnp.stack(new_dconv), jnp.stack(new_fconv), c_kv, k_rope


def setup_inputs(seed: int = 0) -> dict:
    key = jax.random.key(seed)
    keys = iter(jax.random.split(key, 48))
    f32 = jnp.float32
    n_pages = PAST_LEN // PAGE_SIZE
    n_pool = (DEC_BATCH * n_pages * 5) // 4

    def normal(shape, scale=1.0):
        return jax.random.normal(next(keys), shape, f32) * scale

    def gain(shape):
        return 1.0 + normal(shape, 0.02)

    perm = jax.random.permutation(next(keys), n_pool)
    dt = jnp.exp(jax.random.uniform(next(keys), (N_A_LAYERS, GDN_HEADS), f32, math.log(1e-3), math.log(1e-1)))
    a_log = jnp.log(jax.random.uniform(next(keys), (N_A_LAYERS, GDN_HEADS), f32, 1.0, 16.0))
    return {
        'x_prompt': normal((BATCH, SEQ, D_MODEL)),
        'x_sample': normal((DEC_BATCH, DEC_SEQ, D_MODEL)),
        'state_delta_S': normal((N_A_LAYERS, DEC_BATCH, GDN_HEADS, GDN_DK, GDN_DV), 0.1),
        'state_delta_conv': normal((N_A_LAYERS, DEC_BATCH, GDN_CONV - 1, GDN_QKV)),
        'state_ffn_conv': normal((DEPTH, DEC_BATCH, FFN_CONV - 1, 2 * D_FF)),
        'cache_kv_latent': normal((n_pool, PAGE_SIZE, MLA_KV_LORA)),
        'cache_k_rope': normal((n_pool, PAGE_SIZE, MLA_ROPE)),
        'page_table': perm[:DEC_BATCH * n_pages].reshape(DEC_BATCH, n_pages).astype(jnp.int32),
        'meta_tokens': normal((N_META, D_MODEL)),
        'a_norm_pre': gain((N_A_LAYERS, D_MODEL)),
        'a_norm_post': gain((N_A_LAYERS, D_MODEL)),
        'a_w_in': normal((N_A_LAYERS, D_MODEL, GDN_IN), D_MODEL ** -0.5),
        'a_conv_w': normal((N_A_LAYERS, GDN_CONV, GDN_QKV), GDN_CONV ** -0.5),
        'a_log': a_log,
        'a_dt_bias': dt + jnp.log(-jnp.expm1(-dt)),
        'a_out_norm': gain((N_A_LAYERS, GDN_DV)),
        'a_w_out': normal((N_A_LAYERS, GDN_VAL, D_MODEL), GDN_VAL ** -0.5),
        'kv_norm': gain((D_MODEL,)),
        'kv_w_a': normal((D_MODEL, MLA_KV_LORA + MLA_ROPE), D_MODEL ** -0.5),
        'kv_a_norm': gain((MLA_KV_LORA,)),
        'kv_w_uk': normal((MLA_KV_LORA, MLA_HEADS, MLA_NOPE), MLA_KV_LORA ** -0.5),
        'kv_w_uv': normal((MLA_KV_LORA, MLA_HEADS, MLA_V), MLA_KV_LORA ** -0.5),
        'b_norm_pre': gain((N_B_LAYERS, D_MODEL)),
        'b_norm_post': gain((N_B_LAYERS, D_MODEL)),
        'b_w_q_a': normal((N_B_LAYERS, D_MODEL, MLA_Q_LORA), D_MODEL ** -0.5),
        'b_q_a_norm': gain((N_B_LAYERS, MLA_Q_LORA)),
        'b_w_q_b': normal((N_B_LAYERS, MLA_Q_LORA, MLA_HEADS * (MLA_NOPE + MLA_ROPE)), MLA_Q_LORA ** -0.5),
        'b_w_out': normal((N_B_LAYERS, MLA_HEADS * MLA_V, D_MODEL), (MLA_HEADS * MLA_V) ** -0.5),
        'f_norm_pre': gain((DEPTH, D_MODEL)),
        'f_norm_post': gain((DEPTH, D_MODEL)),
        'f_w_up': normal((DEPTH, D_MODEL, 2 * D_FF), D_MODEL ** -0.5),
        'f_conv_w': normal((DEPTH, FFN_CONV, 2 * D_FF), FFN_CONV ** -0.5),
        'f_conv_b': normal((DEPTH, 2 * D_FF), 0.01),
        'f_w_down': normal((DEPTH, D_FF, D_MODEL), D_FF ** -0.5),
    }


def reference(x_prompt, x_sample, state_delta_S, state_delta_conv, state_ffn_conv, cache_kv_latent, cache_k_rope,
              page_table, meta_tokens, a_norm_pre, a_norm_post, a_w_in, a_conv_w, a_log, a_dt_bias, a_out_norm,
              a_w_out, kv_norm, kv_w_a, kv_a_norm, kv_w_uk, kv_w_uv, b_norm_pre, b_norm_post, b_w_q_a, b_q_a_norm,
              b_w_q_b, b_w_out, f_norm_pre, f_norm_post, f_w_up, f_conv_w, f_conv_b, f_w_down):
    p = {
        'a_norm_pre': a_norm_pre, 'a_norm_post': a_norm_post, 'a_w_in': a_w_in, 'a_conv_w': a_conv_w,
        'a_log': a_log, 'a_dt_bias': a_dt_bias, 'a_out_norm': a_out_norm, 'a_w_out': a_w_out,
        'kv_norm': kv_norm, 'kv_w_a': kv_w_a, 'kv_a_norm': kv_a_norm, 'kv_w_uk': kv_w_uk, 'kv_w_uv': kv_w_uv,
        'b_norm_pre': b_norm_pre, 'b_norm_post': b_norm_post, 'b_w_q_a': b_w_q_a, 'b_q_a_norm': b_q_a_norm,
        'b_w_q_b': b_w_q_b, 'b_w_out': b_w_out,
        'f_norm_pre': f_norm_pre, 'f_norm_post': f_norm_post, 'f_w_up': f_w_up, 'f_conv_w': f_conv_w,
        'f_conv_b': f_conv_b, 'f_w_down': f_w_down,
    }
    bp = x_prompt.shape[0]
    L = N_META + x_prompt.shape[1]
    xp = jnp.concatenate([jnp.broadcast_to(meta_tokens.astype(x_prompt.dtype)[None], (bp, N_META, D_MODEL)),
                          x_prompt], axis=1)
    pos_p = jnp.arange(L, dtype=jnp.int32)
    segs_p = ((0, N_META, N_META), (N_META, L, GDN_CHUNK))
    zS = jnp.zeros((N_A_LAYERS, bp, GDN_HEADS, GDN_DK, GDN_DV), state_delta_S.dtype)
    zdc = jnp.zeros((N_A_LAYERS, bp, GDN_CONV - 1, GDN_QKV), x_prompt.dtype)
    zfc = jnp.zeros((DEPTH, bp, FFN_CONV - 1, 2 * D_FF), x_prompt.dtype)
    yp, p_delta_S, p_delta_conv, p_ffn_conv, p_kv_latent, p_k_rope = trunk(
        xp, pos_p, segs_p, zS, zdc, zfc, prompt_attend, p)

    bs, t_new = x_sample.shape[0], x_sample.shape[1]
    past_len = page_table.shape[1] * cache_kv_latent.shape[1]
    c_past = cache_kv_latent[page_table].reshape(bs, past_len, MLA_KV_LORA)
    k_past = cache_k_rope[page_table].reshape(bs, past_len, MLA_ROPE)
    pos_s = past_len + jnp.arange(t_new, dtype=jnp.int32)
    attend_s = functools.partial(sample_attend, c_past=c_past, k_past=k_past)
    ys, s_delta_S, s_delta_conv, s_ffn_conv, s_kv_latent, s_k_rope = trunk(
        x_sample, pos_s, ((0, t_new, t_new),), state_delta_S, state_delta_conv, state_ffn_conv, attend_s, p)

    y_prompt = yp[:, N_META:]
    return (y_prompt, ys, p_delta_S, p_delta_conv, p_ffn_conv, p_kv_latent, p_k_rope,
            s_delta_S, s_delta_conv, s_ffn_conv, s_kv_latent, s_k_rope)
```

```python
import math
from contextlib import ExitStack

import numpy as np
import concourse.bass as bass
import concourse.mybir as mybir
from concourse.bass_utils import run_bass_kernel_spmd

F32 = mybir.dt.float32
BF16 = mybir.dt.bfloat16
I32 = mybir.dt.int32
AF = mybir.ActivationFunctionType
ALU = mybir.AluOpType
AX = mybir.AxisListType

ENGS = ("pe", "act", "dve", "pool", "sp")
STRICT = True
NDSEM = {"pe": 1, "act": 1, "dve": 1, "pool": 24, "sp": 40}


class SemState:
    def __init__(self, nc, es):
        self.esem = {e: es.enter_context(nc.semaphore("sem_" + e)) for e in ENGS}
        self.dsem = {}
        for e in ("pool", "sp"):
            for j in range(NDSEM[e]):
                self.dsem[(e, j)] = es.enter_context(nc.semaphore("dsem_%s_%d" % (e, j)))
        self.ecnt = {e: 0 for e in ENGS}
        self.dma_n = {e: 0 for e in ENGS}
        self.barrier = 0


class Prog:
    def __init__(self, nc, G, strict_same_engine=STRICT):
        self.nc = nc
        self.G = G
        self.strict = strict_same_engine
        self.ops = []
        self.per_eng = {e: [] for e in ENGS}
        self.res = {}
        self.dma_n = dict(G.dma_n)

    def op(self, eng, fn, r=(), w=(), dma=None):
        oid = len(self.ops)
        deps = set()
        for k in r:
            st = self.res.get(k)
            if st is not None and st[0] is not None:
                deps.add(st[0])
        for k in w:
            st = self.res.get(k)
            if st is not None:
                if st[0] is not None:
                    deps.add(st[0])
                deps.update(st[1].values())
                deps.update(st[2])
        for k in r:
            st = self.res.get(k)
            if st is None:
                st = self.res[k] = [None, {}, []]
            if dma is None:
                st[1][eng] = oid
            else:
                st[2].append(oid)
        for k in w:
            self.res[k] = [oid, {}, []]
        keep = []
        for d in deps:
            p = self.ops[d]
            if p["dma"] is None and p["eng"] == eng and (eng == "pe" or not self.strict):
                continue
            keep.append(d)
        val = None
        if dma is not None:
            i = self.dma_n[eng]
            self.dma_n[eng] = i + 1
            dma = (eng, i % NDSEM[eng])
            val = 16 * (i // NDSEM[eng] + 1)
        self.ops.append(dict(eng=eng, fn=fn, deps=keep, dma=dma, dval=val, sig=False, cnt=None))
        self.per_eng[eng].append(oid)
        return oid

    def emit(self):
        nc = self.nc
        ops = self.ops
        G = self.G
        for o in ops:
            for d in o["deps"]:
                ops[d]["sig"] = True
        final = {}
        for e in ENGS:
            if e != "sp":
                comp = [oid for oid in self.per_eng[e] if ops[oid]["dma"] is None]
                if comp:
                    ops[comp[-1]]["sig"] = True
            c = G.ecnt[e]
            for oid in self.per_eng[e]:
                o = ops[oid]
                if o["dma"] is None and o["sig"]:
                    c += 1
                    o["cnt"] = c
            final[e] = c
        final["sp"] += 1
        dtot = {}
        for e in ("pool", "sp"):
            for j in range(NDSEM[e]):
                if self.dma_n[e] > j:
                    dtot[(e, j)] = 16 * ((self.dma_n[e] - 1 - j) // NDSEM[e] + 1)
        esem, dsem = G.esem, G.dsem
        with nc.Block() as block:

            def run(e, eng):
                seen = {}
                if G.barrier > 0:
                    eng.wait_ge(esem["sp"], G.barrier)
                for oid in self.per_eng[e]:
                    o = ops[oid]
                    need = {}
                    for d in o["deps"]:
                        p = ops[d]
                        if p["dma"] is not None:
                            key, v = ("d", p["dma"]), p["dval"]
                        else:
                            key, v = ("e", p["eng"]), p["cnt"]
                        if v > need.get(key, 0):
                            need[key] = v
                    for key, v in need.items():
                        if seen.get(key, 0) >= v:
                            continue
                        seen[key] = v
                        eng.wait_ge(dsem[key[1]] if key[0] == "d" else esem[key[1]], v)
                    if o["dma"] is not None and o["dval"] > 16:
                        key = ("d", o["dma"])
                        if seen.get(key, 0) < o["dval"] - 16:
                            seen[key] = o["dval"] - 16
                            eng.wait_ge(dsem[o["dma"]], o["dval"] - 16)
                    ins = o["fn"](eng)
                    if o["dma"] is not None:
                        ins.then_inc(dsem[o["dma"]], 16)
                    elif o["sig"]:
                        ins.then_inc(esem[e], 1)
                if e == "sp":
                    for k, tot in dtot.items():
                        eng.wait_ge(dsem[k], tot)
                    for e2 in ENGS:
                        if e2 != "sp" and final[e2] > 0:
                            eng.wait_ge(esem[e2], final[e2])
                    eng.nop().then_inc(esem["sp"], 1)

            @block.tensor
            def _(eng):
                run("pe", eng)

            @block.scalar
            def _(eng):
                run("act", eng)

            @block.vector
            def _(eng):
                run("dve", eng)

            @block.gpsimd
            def _(eng):
                run("pool", eng)

            @block.sync
            def _(eng):
                run("sp", eng)
        if DEBUG_SCRATCH:
            print('phase ops', {e: len(self.per_eng[e]) for e in ENGS}, 'sig', final, flush=True)
        G.ecnt = final
        G.dma_n = dict(self.dma_n)
        G.barrier = final["sp"]


D = 1024
NQKV = 3072
GIN = 4112
DFF = 2816
DFF2 = 5632
LP = 2064
NB = 2
NS = 16
NPG = 64
EPS = 1e-6
MLA_SCALE = 1.0 / math.sqrt(192.0)
NEG = -1.0e30
DEBUG_SCRATCH = False
HACK_SRC = {}


class Ctx:
    def __init__(self, nc, P):
        self.nc = nc
        self.P = P

    def mm(self, out, lhsT, rhs, start=True, stop=True, r=(), w=()):
        self.P.op("pe", lambda e, a=(out, lhsT, rhs, start, stop): e.matmul(a[0], lhsT=a[1], rhs=a[2], start=a[3], stop=a[4]), r, w)

    def tr(self, out, in_, ident, r=(), w=()):
        self.P.op("pe", lambda e, a=(out, in_, ident): e.transpose(a[0], a[1], a[2]), r, w)

    def act(self, out, in_, func, r=(), w=(), **kw):
        self.P.op("act", lambda e, a=(out, in_, func, kw): e.activation(out=a[0], in_=a[1], func=a[2], **a[3]), r, w)

    def ts(self, eng, out, in0, s1, s2, op0, op1=None, r=(), w=()):
        if op1 is None:
            self.P.op(eng, lambda e, a=(out, in0, s1, op0): e.tensor_scalar(out=a[0], in0=a[1], scalar1=a[2], scalar2=None, op0=a[3]), r, w)
        else:
            self.P.op(eng, lambda e, a=(out, in0, s1, s2, op0, op1): e.tensor_scalar(out=a[0], in0=a[1], scalar1=a[2], scalar2=a[3], op0=a[4], op1=a[5]), r, w)

    def tt(self, eng, out, in0, in1, op, r=(), w=()):
        self.P.op(eng, lambda e, a=(out, in0, in1, op): e.tensor_tensor(out=a[0], in0=a[1], in1=a[2], op=a[3]), r, w)

    def stt(self, out, in0, scalar, in1, op0, op1, r=(), w=()):
        self.P.op("dve", lambda e, a=(out, in0, scalar, in1, op0, op1): e.scalar_tensor_tensor(out=a[0], in0=a[1], scalar=a[2], in1=a[3], op0=a[4], op1=a[5]), r, w)

    def cp(self, eng, out, in_, r=(), w=()):
        if eng == "act":
            self.P.op("act", lambda e, a=(out, in_): e.copy(out=a[0], in_=a[1]), r, w)
        else:
            self.P.op(eng, lambda e, a=(out, in_): e.tensor_copy(out=a[0], in_=a[1]), r, w)

    def red(self, out, in_, op, r=(), w=()):
        self.P.op("dve", lambda e, a=(out, in_, op): e.tensor_reduce(out=a[0], in_=a[1], axis=AX.X, op=a[2]), r, w)

    def recip(self, out, in_, r=(), w=()):
        self.P.op("dve", lambda e, a=(out, in_): e.reciprocal(out=a[0], in_=a[1]), r, w)

    def memset(self, eng, ap, val, w=()):
        self.P.op(eng, lambda e, a=(ap, val): e.memset(a[0], a[1]), (), w)

    def dma(self, eng, out, in_, r=(), w=(), **kw):
        self.P.op(eng, lambda e, a=(out, in_, kw): e.dma_start(out=a[0], in_=a[1], **a[2]), r, w, dma=True)

    def load_w(self, dst, src, KC, N, key, rows=128):
        for k in range(KC):
            for c0 in range(0, N, 2048):
                c1 = min(N, c0 + 2048)
                self.dma("pool", dst[:rows, k, c0:c1], src[k * rows:(k + 1) * rows, c0:c1], w=[key + "%d" % k] if c0 == 0 else [key + "%d_%d" % (k, c0)])

    def load_w_keys(self, key, KC, N):
        ks = []
        for k in range(KC):
            for c0 in range(0, N, 2048):
                ks.append(key + "%d" % k if c0 == 0 else key + "%d_%d" % (k, c0))
        return ks


class Tile:
    def __init__(self, kind, n, pos, first, last, b=0, i=0, s=0):
        self.kind, self.n, self.pos, self.first, self.last, self.b, self.i, self.s = kind, n, pos, first, last, b, i, s


def make_tiles():
    tiles = []
    for b in range(NB):
        for i in range(17):
            n = 16 if i == 0 else 128
            pos = 0 if i == 0 else 16 + 128 * (i - 1)
            tiles.append(Tile("p", n, pos, i == 0, i == 16, b=b, i=i))
    for s in range(NS):
        tiles.append(Tile("s", 1, 8192, True, True, s=s))
    return tiles


def build_program(phases=(1, 2, 3, 4), debug_tiles=None):
    nc = bass.Bass("TRN2", target_bir_lowering=False)

    def din(name, shape, dtype=F32):
        return nc.dram_tensor(name, list(shape), dtype, kind="ExternalInput").ap()

    def dout(name, shape):
        return nc.dram_tensor(name, list(shape), F32, kind="ExternalOutput").ap()

    def dscr(name, shape):
        return nc.dram_tensor(name, list(shape), F32, kind="ExternalOutput" if DEBUG_SCRATCH else "Internal").ap()

    I = {}
    for name, shape in [
        ("x_prompt", (NB, 2048, D)), ("x_sample", (NS, D)), ("state_delta_S", (NS, 8, 128, 128)),
        ("state_delta_conv", (NS, 3, NQKV)), ("state_ffn_conv", (2, NS, 2, DFF2)),
        ("cache_kv_latent", (10240 * 128, 256)), ("cache_k_rope", (10240 * 128, 64)),
        ("meta_tokens", (16, D)), ("a_norm_pre", (1, D)), ("a_norm_post", (1, D)), ("a_w_in", (D, GIN)),
        ("a_conv_w", (4, NQKV)), ("a_log", (1, 8)), ("a_dt_bias", (1, 8)), ("a_out_norm", (1, 128)),
        ("a_w_out", (D, D)), ("kv_norm", (1, D)), ("kv_w_a", (D, 320)), ("kv_a_norm", (1, 256)),
        ("kv_w_uk", (256, 1024)), ("kv_w_uv", (256, 1024)), ("b_norm_pre", (1, D)), ("b_norm_post", (1, D)),
        ("b_w_q_a", (D, 384)), ("b_q_a_norm", (1, 384)), ("b_w_q_b", (384, 1536)), ("b_w_out", (D, D)),
        ("f_norm_pre", (2, D)), ("f_norm_post", (2, D)), ("f_w_up", (2, D, DFF2)), ("f_conv_w", (2, 3, DFF2)),
        ("f_conv_b", (2, DFF2)), ("f_w_down", (2, DFF, D)),
        ("c_ident", (128, 128)), ("c_triu", (128, 128)), ("c_lstrict", (128, 128)), ("c_lincl", (128, 128)),
        ("c_negmask", (128, 128)), ("c_rope", (2065, 64)), ("c_iota", (128, 1)),
    ]:
        I[name] = din(name, shape)
    I["page_table"] = din("page_table", (NS, NPG), I32)
    O = {}
    for name, shape in [
        ("y_prompt", (NB, 2048, D)), ("y_sample", (NS, D)), ("p_delta_S", (NB, 8, 128, 128)),
        ("p_delta_conv", (NB, 3, NQKV)), ("p_ffn_conv", (2, NB, 2, DFF2)), ("p_kv_latent", (NB, LP, 256)),
        ("p_k_rope", (NB, LP, 64)), ("s_delta_S", (NS, 8, 128, 128)), ("s_delta_conv", (NS, 3, NQKV)),
        ("s_ffn_conv", (2, NS, 2, DFF2)), ("s_kv_latent", (NS, 256)), ("s_k_rope", (NS, 64)),
    ]:
        O[name] = dout(name, shape)
    xsp = {k: dscr("xsp%d" % k, (NB, LP, D)) for k in (1, 2, 3)}
    xss = {k: dscr("xss%d" % k, (NS, D)) for k in (1, 2, 3)}
    tiles = make_tiles() if debug_tiles is None else debug_tiles(make_tiles())
    ges = ExitStack()
    GS = SemState(nc, ges)

    def x_in(ph, t):
        if t.kind == "p":
            if ph == 1:
                return I["meta_tokens"] if t.i == 0 else I["x_prompt"][t.b, 128 * (t.i - 1):128 * t.i, :]
            return xsp[HACK_SRC.get(ph, ph - 1)][t.b, t.pos:t.pos + t.n, :]
        return I["x_sample"][t.s:t.s + 1, :] if ph == 1 else xss[ph - 1][t.s:t.s + 1, :]

    def x_out(ph, t):
        if t.kind == "p":
            if ph == 4:
                return None if t.i == 0 else O["y_prompt"][t.b, 128 * (t.i - 1):128 * t.i, :]
            return xsp[ph][t.b, t.pos:t.pos + t.n, :]
        return O["y_sample"][t.s:t.s + 1, :] if ph == 4 else xss[ph][t.s:t.s + 1, :]

    class Phase:
        def __init__(self, name):
            self.name = name
            self.es = ExitStack()
            self.P = Prog(nc, GS)
            self.c = Ctx(nc, self.P)
            self.cnt = 0
            c = self.c
            self.ps = self.es.enter_context(nc.psum_tensor("ps_" + name, [128, 4096], F32))
            self.psb = self.ps[:].bitcast(BF16)
            self.ident_f = self.sb("ident_f", [128, 128], F32)
            self.ident_b = self.sb("ident_b", [128, 128], BF16)
            c.dma("sp", self.ident_f[:], I["c_ident"], w=["ident_f"])
            c.dma("pool", self.ident_b[:], I["c_ident"], w=["ident_b"])
            self.x = [self.sb("x%d" % j, [128, D], F32) for j in range(2)]
            self.junk = self.sb("junk", [128, D], F32)
            self.hn = self.sb("hn", [128, D], BF16)
            self.hT = self.sb("hT", [128, 8, 128], BF16)
            self.st = self.sb("st", [128, 16], F32)
            self.tmp = self.sb("tmpx", [128, D], F32)

        def sb(self, name, shape, dtype):
            return self.es.enter_context(nc.sbuf_tensor(self.name + "_" + name, list(shape), dtype))

        def bank(self, k, nb=1):
            return self.ps[:, 512 * k:512 * (k + nb)]

        def bankb(self, k):
            return self.psb[:, 1024 * k:1024 * (k + 1)]

        def gain(self, name, src_row):
            g = self.sb(name, [128, src_row.shape[-1]], F32)
            self.c.dma("sp", g[:], src_row.partition_broadcast(128), w=[name])
            return g

        def load_x(self, ph, t, slot):
            self.c.dma("sp", self.x[slot][:t.n, :], x_in(ph, t), w=["x%d" % slot])

        def rstd(self, src, n, width, col, r):
            c = self.c
            st = self.st
            c.act(self.junk[:n, :width], src, AF.Square, r=r, w=["junk", "st%d" % col], accum_out=st[:n, col:col + 1])
            c.act(st[:n, col:col + 1], st[:n, col:col + 1], AF.Sqrt, r=["st%d" % col], w=["st%d" % col], scale=1.0 / width, bias=EPS)
            c.recip(st[:n, col:col + 1], st[:n, col:col + 1], r=["st%d" % col], w=["st%d" % col])

        def norm_T(self, xap, xkey, gain, gkey, n, bk, width=D, hn=None, hT=None, hkey="hT", col=0):
            c = self.c
            hn = self.hn if hn is None else hn
            hT = self.hT if hT is None else hT
            KC = width // 128
            self.rstd(xap, n, width, col, [xkey])
            c.stt(hn[:n, :width], xap, self.st[:n, col:col + 1], gain[:n, :width], ALU.mult, ALU.mult, r=[xkey, "st%d" % col, gkey], w=["hn" + hkey])
            pb = self.bankb(bk)
            for k in range(KC):
                c.tr(pb[:, k * 128:k * 128 + n], hn[:n, k * 128:(k + 1) * 128], self.ident_b[:n, :n], r=["hn" + hkey, "ident_b"], w=["bank%d" % bk])
            c.cp("act", hT[:, 0:KC, :n], pb[:, 0:KC * 128].rearrange("p (k t) -> p k t", t=128)[:, :, :n], r=["bank%d" % bk], w=[hkey])

        def post_norm_res(self, obk, n, gain, gkey, xslot, col=1):
            c = self.c
            o = self.bank(obk, 2)[:n, :]
            okeys = ["bank%d" % obk, "bank%d" % (obk + 1)]
            self.rstd(o, n, D, col, okeys)
            c.stt(self.tmp[:n, :], o, self.st[:n, col:col + 1], gain[:n, :], ALU.mult, ALU.mult, r=okeys + ["st%d" % col, gkey], w=["tmpx"])
            c.tt("dve", self.x[xslot][:n, :], self.x[xslot][:n, :], self.tmp[:n, :], ALU.add, r=["tmpx", "x%d" % xslot], w=["x%d" % xslot])

        def store_x(self, ph, t, slot):
            dst = x_out(ph, t)
            if dst is not None:
                self.c.dma("sp", dst, self.x[slot][:t.n, :], r=["x%d" % slot], w=["xout"])

        def finish(self):
            self.P.emit()
            self.es.close()

    def phase_gdn():
        ph = Phase("g")
        c = ph.c
        w_in = ph.sb("w_in", [128, 8, GIN], BF16)
        w_out = ph.sb("w_out", [128, 8, D], BF16)
        c.load_w(w_in, I["a_w_in"], 8, GIN, "w_in")
        c.load_w(w_out, I["a_w_out"], 8, D, "w_out")
        K_WIN = c.load_w_keys("w_in", 8, GIN)
        K_WOUT = c.load_w_keys("w_out", 8, D)
        g_pre = ph.gain("g_pre", I["a_norm_pre"])
        g_post = ph.gain("g_post", I["a_norm_post"])
        g_on = ph.gain("g_on", I["a_out_norm"])
        alog = ph.gain("alog", I["a_log"])
        dtb = ph.gain("dtb", I["a_dt_bias"])
        negA = ph.sb("negA", [128, 8], F32)
        c.act(negA[:], alog[:], AF.Exp, r=["alog"], w=["negA"])
        c.ts("dve", negA[:], negA[:], -1.0, None, ALU.mult, r=["negA"], w=["negA"])
        triu = ph.sb("triu", [128, 128], F32)
        lstrict = ph.sb("lstrict", [128, 128], F32)
        lincl = ph.sb("lincl", [128, 128], F32)
        ones = ph.sb("ones", [128, 128], F32)
        nones = ph.sb("nones", [128, 128], F32)
        c.dma("sp", triu[:], I["c_triu"], w=["triu"])
        c.dma("sp", lstrict[:], I["c_lstrict"], w=["lstrict"])
        c.dma("sp", lincl[:], I["c_lincl"], w=["lincl"])
        c.memset("dve", ones[:], 1.0, w=["ones"])
        c.memset("dve", nones[:], -1.0, w=["nones"])
        cwrow = ph.sb("cwrow", [96, 128], F32)
        cw = ph.sb("cw", [128, 24, 4], F32)
        c.dma("sp", cwrow[:], I["a_conv_w"].rearrange("j (c p) -> (j c) p", p=128), w=["cwrow"])
        c.tr(ph.bank(0)[:, 0:96], cwrow[:96, :], ph.ident_f[:96, :96], r=["cwrow", "ident_f"], w=["bank0"])
        c.cp("dve", cw[:], ph.bank(0)[:, 0:96].rearrange("p (j c) -> p c j", j=4), r=["bank0"], w=["cw"])

        xbuf = ph.sb("xbuf", [128, 24, 131], F32)
        ybuf = [ph.sb("ybuf%d" % j, [128, 128], F32) for j in range(8)]
        qkvT = ph.sb("qkvT", [128, 24, 128], BF16)
        tok = ph.sb("tok", [128, 24, 128], BF16)
        zs = ph.tmp
        gb = ph.sb("gb", [128, 64], F32)
        srow = ph.sb("srow", [72, 128], F32)
        crow = ph.sb("crow", [4, 1024], F32)
        kn = ph.sb("kn", [128, 8, 128], BF16)
        kbg = ph.sb("kbg", [128, 8, 128], BF16)
        ktl = ph.sb("ktl", [128, 8, 128], BF16)
        qn = ph.sb("qn", [128, 8, 128], BF16)
        qd = ph.sb("qd", [128, 8, 128], BF16)
        vb = ph.sb("vb", [128, 8, 128], BF16)
        knT = ph.sb("knT", [128, 8, 128], BF16)
        qnT = ph.sb("qnT", [128, 8, 128], BF16)
        qdT = ph.sb("qdT", [128, 8, 128], BF16)
        Gh = [ph.sb("Gh%d" % j, [128, 128], F32) for j in range(2)]
        Em = ph.sb("Em", [128, 4, 128], F32)
        Ee = ph.sb("Ee", [128, 4, 128], F32)
        t4 = ph.sb("t4", [128, 4, 128], F32)
        Am = ph.sb("Am", [128, 4, 128], F32)
        At = ph.sb("At", [128, 4, 128], F32)
        attn = ph.sb("attn", [128, 4, 128], BF16)
        attnT = ph.sb("attnT", [128, 4, 128], BF16)
        Pb = [ph.sb("Pb%d" % j, [128, 4, 128], F32) for j in range(2)]
        Qb = [ph.sb("Qb%d" % j, [128, 4, 128], F32) for j in range(2)]
        Xb = [ph.sb("Xb%d" % j, [128, 4, 128], F32) for j in range(2)]
        Xf = ph.sb("Xf", [128, 4, 128], BF16)
        nwT = ph.sb("nwT", [128, 4, 128], BF16)
        vnew = ph.sb("vnew", [128, 4, 128], BF16)
        S = ph.sb("S", [128, 8, 128], F32)
        Sb = ph.sb("Sb", [128, 8, 128], BF16)
        otok = ph.sb("otok", [128, 8, 128], F32)
        og = ph.sb("og", [128, D], BF16)
        oT = ph.sb("oT", [128, 8, 128], BF16)

        BETA, G, GC, GL, EG, EGL, EGLAST, RINV, FKBG, FKT, FQN, FQD, OSS = 0, 8, 16, 24, 32, 40, 48, 56, 72, 80, 88, 96, 104
        gb2 = ph.sb("gb2", [128, 112], F32)

        def front_a(ti):
            t = tiles[ti]
            n = t.n
            xs = ti % 2
            ph.load_x(1, t, xs)
            if t.first:
                if t.kind == "p":
                    c.memset("pool", xbuf[:, :, 0:3], 0.0, w=["xbuf%d" % g for g in range(6)])
                else:
                    c.dma("sp", srow[:], I["state_delta_conv"][t.s].rearrange("j (c p) -> (j c) p", p=128), w=["srow"])
                    c.tr(ph.bank(0)[:, 0:72], srow[:72, :], ph.ident_f[:72, :72], r=["srow", "ident_f"], w=["bank0"])
                    c.cp("dve", xbuf[:, :, 0:3], ph.bank(0)[:, 0:72].rearrange("p (j c) -> p c j", j=3), r=["bank0"], w=["xbuf%d" % g for g in range(6)])
                    c.dma("sp", O["s_delta_conv"][t.s, 0:2, :], I["state_delta_conv"][t.s, 1:3, :], w=["sdc_out"])
            ph.norm_T(ph.x[xs][:n, :], "x%d" % xs, g_pre, "g_pre", n, 1)

        def front_b(ti, cg):
            t = tiles[ti]
            n = t.n
            bk = cg % 2
            q_ = cg % 2
            for j in range(4):
                ch = cg * 4 + j
                for k in range(8):
                    c.mm(ph.bank(bk)[:, j * 128:j * 128 + n], w_in[:, k, ch * 128:(ch + 1) * 128], ph.hT[:, k, :n], start=(k == 0), stop=(k == 7), r=["hT"] + K_WIN, w=["bank%d" % bk])
            c.cp("act", xbuf[:, cg * 4:cg * 4 + 4, 3:3 + n], ph.bank(bk).rearrange("p (j t) -> p j t", t=128)[:, :, :n], r=["bank%d" % bk], w=["xbuf%d" % cg])
            for j in range(4):
                ch = cg * 4 + j
                c.act(ybuf[q_ * 4 + j][:, :n], ph.bank(bk)[:, j * 128:j * 128 + n], AF.Identity, r=["bank%d" % bk, "cw"], w=["ybuf%d" % (q_ * 4 + j)], scale=cw[:, ch, 3:4])
            for tap in range(3):
                for j in range(4):
                    ch = cg * 4 + j
                    yk = "ybuf%d" % (q_ * 4 + j)
                    c.stt(ybuf[q_ * 4 + j][:, :n], xbuf[:, ch, tap:tap + n], cw[:, ch, tap:tap + 1], ybuf[q_ * 4 + j][:, :n], ALU.mult, ALU.add, r=["xbuf%d" % cg, "cw", yk], w=[yk])
            for j in range(4):
                ch = cg * 4 + j
                c.act(qkvT[:, ch, :n], ybuf[q_ * 4 + j][:, :n], AF.Silu, r=["ybuf%d" % (q_ * 4 + j)], w=["qkvT%d" % (ch // 8)])
            if not t.last:
                c.cp("act", xbuf[:, cg * 4:cg * 4 + 4, 0:3], xbuf[:, cg * 4:cg * 4 + 4, n:n + 3], r=["xbuf%d" % cg], w=["xbuf%d" % cg])

        pending = []

        def pump(k=1):
            for _ in range(k):
                if pending:
                    pending.pop(0)()

        front_a(0)
        for cg in range(6):
            front_b(0, cg)
        for ti, t in enumerate(tiles):
            n = t.n
            xs = ti % 2
            xk = "x%d" % xs
            x = ph.x[xs]
            if t.first:
                if t.kind == "p":
                    c.memset("pool", S[:], 0.0, w=["S"])
                    c.memset("pool", Sb[:], 0.0, w=["Sb"])
                else:
                    c.dma("sp", S[:], I["state_delta_S"][t.s].rearrange("h k v -> k h v"), w=["S"])
                    c.cp("pool", Sb[:], S[:], r=["S"], w=["Sb"])
            if ti + 1 < len(tiles):
                pending.append(lambda ti=ti: front_a(ti + 1))
                for cg in range(6):
                    pending.append(lambda ti=ti, cg=cg: front_b(ti + 1, cg))
            for hf in range(2):
                for k in range(8):
                    c.mm(ph.bank(hf)[:n, :], ph.hT[:, k, :n], w_in[:, k, 3072 + hf * 512:3072 + (hf + 1) * 512], start=(k == 0), stop=(k == 7), r=["hT"] + K_WIN, w=["bank%d" % hf])
            c.act(zs[:n, :], ph.bank(0, 2)[:n, :], AF.Silu, r=["bank0", "bank1"], w=["tmpx"])
            for k in range(8):
                c.mm(ph.bank(4)[:n, 0:16], ph.hT[:, k, :n], w_in[:, k, 4096:4112], start=(k == 0), stop=(k == 7), r=["hT"] + K_WIN, w=["bank4"])
            c.act(gb2[:n, BETA:BETA + 8], ph.bank(4)[:n, 8:16], AF.Sigmoid, r=["bank4"], w=["beta"])
            c.tt("dve", gb2[:n, G:G + 8], ph.bank(4)[:n, 0:8], dtb[:n, :], ALU.add, r=["bank4", "dtb"], w=["g"])
            c.act(gb2[:n, G:G + 8], gb2[:n, G:G + 8], AF.Exp, r=["g"], w=["g"])
            c.act(gb2[:n, G:G + 8], gb2[:n, G:G + 8], AF.Ln, r=["g"], w=["g"], bias=1.0)
            c.tt("dve", gb2[:n, G:G + 8], gb2[:n, G:G + 8], negA[:n, :], ALU.mult, r=["g", "negA"], w=["g"])
            if t.last:
                M = min(n, 3)
                for rd in range(3):
                    for hf in range(2):
                        for k in range(8):
                            c.mm(ph.bank(hf)[:M, :], ph.hT[:, k, n - M:n], w_in[:, k, rd * 1024 + hf * 512:rd * 1024 + (hf + 1) * 512], start=(k == 0), stop=(k == 7), r=["hT"] + K_WIN, w=["bank%d" % hf])
                    c.cp("act", crow[:M, :], ph.bank(0, 2)[:M, :], r=["bank0", "bank1"], w=["crow"])
                    dst = O["p_delta_conv"][t.b, 3 - M:3, rd * 1024:(rd + 1) * 1024] if t.kind == "p" else O["s_delta_conv"][t.s, 3 - M:3, rd * 1024:(rd + 1) * 1024]
                    c.dma("sp", dst, crow[:M, :], r=["crow"], w=["dc_out"])
            for g3 in range(3):
                pb = ph.bankb(5 + g3)
                for j in range(8):
                    ch = g3 * 8 + j
                    c.tr(pb[:n, j * 128:(j + 1) * 128], qkvT[:, ch, :n], ph.ident_b[:, :], r=["qkvT%d" % g3, "ident_b"], w=["bank%d" % (5 + g3)])
                c.cp("dve" if g3 == 1 else "act", tok[:n, g3 * 8:(g3 + 1) * 8, :], pb[:n, :].rearrange("p (j d) -> p j d", d=128), r=["bank%d" % (5 + g3)], w=["tok%d" % g3])
            for ch in range(16):
                c.act(ph.junk[:n, 0:128], tok[:n, ch, :], AF.Square, r=["tok%d" % (ch // 8)], w=["junk", "rinv"], accum_out=gb2[:n, RINV + ch:RINV + ch + 1])
            c.act(gb2[:n, RINV:RINV + 16], gb2[:n, RINV:RINV + 16], AF.Sqrt, r=["rinv"], w=["rinv"], bias=EPS)
            c.recip(gb2[:n, RINV:RINV + 16], gb2[:n, RINV:RINV + 16], r=["rinv"], w=["rinv"])
            c.mm(ph.bank(4)[:n, 16:24], triu[:n, :n], gb2[:n, G:G + 8], r=["triu", "g"], w=["bank4"])
            c.mm(ph.bank(4)[:, 24:32], ones[:n, :], gb2[:n, G:G + 8], r=["ones", "g"], w=["bank4"])
            c.cp("dve", gb2[:n, GC:GC + 8], ph.bank(4)[:n, 16:24], r=["bank4"], w=["gc"])
            c.cp("dve", gb2[:, GL:GL + 8], ph.bank(4)[:, 24:32], r=["bank4"], w=["gl"])
            c.act(gb2[:n, EG:EG + 8], gb2[:n, GC:GC + 8], AF.Exp, r=["gc"], w=["eg"])
            c.tt("dve", gb2[:n, EGL:EGL + 8], gb2[:n, GL:GL + 8], gb2[:n, GC:GC + 8], ALU.subtract, r=["gl", "gc"], w=["egl"])
            c.act(gb2[:n, EGL:EGL + 8], gb2[:n, EGL:EGL + 8], AF.Exp, r=["egl"], w=["egl"])
            c.act(gb2[:, EGLAST:EGLAST + 8], gb2[:, GL:GL + 8], AF.Exp, r=["gl"], w=["eglast"])
            c.tt("dve", gb2[:n, FKBG:FKBG + 8], gb2[:n, RINV + 8:RINV + 16], gb2[:n, BETA:BETA + 8], ALU.mult, r=["rinv", "beta"], w=["fkbg"])
            c.tt("dve", gb2[:n, FKBG:FKBG + 8], gb2[:n, FKBG:FKBG + 8], gb2[:n, EG:EG + 8], ALU.mult, r=["fkbg", "eg"], w=["fkbg"])
            c.tt("dve", gb2[:n, FKT:FKT + 8], gb2[:n, RINV + 8:RINV + 16], gb2[:n, EGL:EGL + 8], ALU.mult, r=["rinv", "egl"], w=["fkt"])
            c.ts("dve", gb2[:n, FQN:FQN + 8], gb2[:n, RINV:RINV + 8], 128.0 ** -0.5, None, ALU.mult, r=["rinv"], w=["fqn"])
            c.tt("dve", gb2[:n, FQD:FQD + 8], gb2[:n, FQN:FQN + 8], gb2[:n, EG:EG + 8], ALU.mult, r=["fqn", "eg"], w=["fqd"])

            def bc(col):
                return gb2[:n, col:col + 8].unsqueeze(2).to_broadcast([n, 8, 128])
            c.tt("dve", kn[:n], tok[:n, 8:16, :], bc(RINV + 8), ALU.mult, r=["tok1", "rinv"], w=["kn"])
            c.tt("dve", kbg[:n], tok[:n, 8:16, :], bc(FKBG), ALU.mult, r=["tok1", "fkbg"], w=["kbg"])
            c.tt("pool", ktl[:n], tok[:n, 8:16, :], bc(FKT), ALU.mult, r=["tok1", "fkt"], w=["ktl"])
            c.tt("dve", qn[:n], tok[:n, 0:8, :], bc(FQN), ALU.mult, r=["tok0", "fqn"], w=["qn"])
            c.tt("pool", qd[:n], tok[:n, 0:8, :], bc(FQD), ALU.mult, r=["tok0", "fqd"], w=["qd"])
            c.tt("pool", vb[:n], tok[:n, 16:24, :], bc(BETA), ALU.mult, r=["tok2", "beta"], w=["vb"])
            for bi, (src, dst, sk, dk_) in enumerate([(kn, knT, "kn", "knT"), (qn, qnT, "qn", "qnT"), (qd, qdT, "qd", "qdT")]):
                bk = 5 + bi
                pb = ph.bankb(bk)
                for h in range(8):
                    c.tr(pb[:, h * 128:h * 128 + n], src[:n, h, :], ph.ident_b[:n, :n], r=[sk, "ident_b"], w=["bank%d" % bk])
                c.cp("act" if bi != 1 else "dve", dst[:, :, :n], pb[:, :].rearrange("p (h t) -> p h t", t=128)[:, :, :n], r=["bank%d" % bk], w=[dk_])
            nlev = 0
            while (1 << (nlev + 1)) < n:
                nlev += 1
            for hg in range(2):
                hs = [hg * 4 + j for j in range(4)]
                for j, h in enumerate(hs):
                    G_ = Gh[j % 2]
                    gk = "Gh%d" % (j % 2)
                    c.ts("dve", G_[:n, :n], triu[:n, :n], gb2[:n, G + h:G + h + 1], None, ALU.mult, r=["triu", "g"], w=[gk])
                    c.mm(ph.bank(2)[:n, j * 128:j * 128 + n], G_[:n, :n], ones[:n, :n], start=True, stop=False, r=[gk, "ones"], w=["bank2"])
                    c.mm(ph.bank(2)[:n, j * 128:j * 128 + n], nones[:n, :n], G_[:n, :n], start=False, stop=True, r=[gk, "nones"], w=["bank2"])
                    c.mm(ph.bank(3)[:n, j * 128:j * 128 + n], knT[:, h, :n], knT[:, h, :n], r=["knT"], w=["bank3"])
                    c.mm(ph.bank(4)[:n, j * 128:j * 128 + n], qnT[:, h, :n], knT[:, h, :n], r=["qnT", "knT"], w=["bank4"])

                def v4(bk):
                    return ph.bank(bk)[:n, :].rearrange("p (j t) -> p j t", t=128)[:, :, :n]
                c.ts("dve", Em[:n, :, :n], v4(2), 0.0, None, ALU.min, r=["bank2"], w=["Em"])
                c.act(Ee[:n, :, :n], Em[:n, :, :n], AF.Exp, r=["Em"], w=["Ee"])
                c.tt("dve", t4[:n, :, :n], v4(3), Ee[:n, :, :n], ALU.mult, r=["bank3", "Ee"], w=["t4"])
                for j, h in enumerate(hs):
                    c.stt(Am[:n, j, :n], t4[:n, j, :n], gb2[:n, BETA + h:BETA + h + 1], lstrict[:n, :n], ALU.mult, ALU.mult, r=["t4", "beta", "lstrict"], w=["Am"])
                c.tt("dve", t4[:n, :, :n], v4(4), Ee[:n, :, :n], ALU.mult, r=["bank4", "Ee", "Am"], w=["t4"])
                c.tt("pool", attn[:n, :, :n], t4[:n, :, :n], lincl[:n, :n].unsqueeze(1).to_broadcast([n, 4, n]), ALU.mult, r=["t4", "lincl"], w=["attn"])
                pb = ph.bankb(6)
                for j in range(4):
                    c.tr(ph.bank(5)[:n, j * 128:j * 128 + n], Am[:n, j, :n], ph.ident_f[:n, :n], r=["Am", "ident_f"], w=["bank5"])
                    c.tr(pb[:n, j * 128:j * 128 + n], attn[:n, j, :n], ph.ident_b[:n, :n], r=["attn", "ident_b"], w=["bank6"])
                c.cp("act", At[:n, :, :n], v4(5), r=["bank5"], w=["At"])
                c.cp("act", attnT[:n, :, :n], pb[:n, 0:512].rearrange("p (j t) -> p j t", t=128)[:, :, :n], r=["bank6"], w=["attnT"])
                c.tt("dve", Xb[0][:n, :, :n], ph.ident_f[:n, :n].unsqueeze(1).to_broadcast([n, 4, n]), At[:n, :, :n], ALU.subtract, r=["ident_f", "At"], w=["Xb0"])
                Pc, Qc, Pk, Qk = At, Am, "At", "Am"
                xi = 0
                for lv in range(1, nlev + 1):
                    lastl = lv == nlev
                    Qn_, Qnk = Qb[lv % 2], "Qb%d" % (lv % 2)
                    Pn_, Pnk = Pb[lv % 2], "Pb%d" % (lv % 2)
                    for j in range(4):
                        c.mm(ph.bank(6)[:n, j * 128:j * 128 + n], Pc[:n, j, :n], Qc[:n, j, :n], r=[Pk, Qk], w=["bank6"])
                        if not lastl:
                            c.mm(ph.bank(7)[:n, j * 128:j * 128 + n], Qc[:n, j, :n], Pc[:n, j, :n], r=[Pk, Qk], w=["bank7"])
                    c.cp("act", Qn_[:n, :, :n], v4(6), r=["bank6"], w=[Qnk])
                    if not lastl:
                        c.cp("dve", Pn_[:n, :, :n], v4(7), r=["bank7"], w=[Pnk])
                    for j in range(4):
                        c.mm(ph.bank(2)[:n, j * 128:j * 128 + n], Qn_[:n, j, :n], Xb[xi][:n, j, :n], r=[Qnk, "Xb%d" % xi], w=["bank2"])
                    c.tt("dve", Xb[1 - xi][:n, :, :n], v4(2), Xb[xi][:n, :, :n], ALU.add, r=["bank2", "Xb%d" % xi], w=["Xb%d" % (1 - xi)])
                    xi = 1 - xi
                    Pc, Qc, Pk, Qk = Pn_, Qn_, Pnk, Qnk
                    pump(1)
                c.cp("act", Xf[:n, :, :n], Xb[xi][:n, :, :n], r=["Xb%d" % xi], w=["Xf"])
                X_ = Xf
                Xk = "Xf"
                for j, h in enumerate(hs):
                    c.mm(ph.bank(3)[:, j * 128:j * 128 + n], kbg[:n, h, :], X_[:n, j, :n], r=["kbg", Xk], w=["bank3"])
                c.act(nwT[:, :, :n], ph.bank(3).rearrange("p (j t) -> p j t", t=128)[:, :, :n], AF.Copy, r=["bank3"], w=["nwT"], scale=-1.0)
                for j, h in enumerate(hs):
                    c.mm(ph.bank(4)[:n, j * 128:(j + 1) * 128], X_[:n, j, :n], vb[:n, h, :], start=True, stop=False, r=[Xk, "vb"], w=["bank4"])
                    c.mm(ph.bank(4)[:n, j * 128:(j + 1) * 128], nwT[:, j, :n], Sb[:, h, :], start=False, stop=True, r=["nwT", "Sb"], w=["bank4"])
                c.cp("act", vnew[:n], ph.bank(4)[:n, :].rearrange("p (j d) -> p j d", d=128), r=["bank4"], w=["vnew"])
                for j, h in enumerate(hs):
                    c.mm(ph.bank(5)[:n, j * 128:(j + 1) * 128], qdT[:, h, :n], Sb[:, h, :], start=True, stop=False, r=["qdT", "Sb"], w=["bank5"])
                    c.mm(ph.bank(5)[:n, j * 128:(j + 1) * 128], attnT[:n, j, :n], vnew[:n, j, :], start=False, stop=True, r=["attnT", "vnew"], w=["bank5"])
                c.cp("dve", otok[:n, hg * 4:hg * 4 + 4, :], ph.bank(5)[:n, :].rearrange("p (j d) -> p j d", d=128), r=["bank5"], w=["otok"])
                for j, h in enumerate(hs):
                    c.mm(ph.bank(6)[:, j * 128:(j + 1) * 128], ktl[:n, h, :], vnew[:n, j, :], r=["ktl", "vnew"], w=["bank6"])
                for j, h in enumerate(hs):
                    c.stt(S[:, h, :], S[:, h, :], gb2[:, EGLAST + h:EGLAST + h + 1], ph.bank(6)[:, j * 128:(j + 1) * 128], ALU.mult, ALU.add, r=["S", "eglast", "bank6", "Sb"], w=["S"])
                c.cp("pool", Sb[:, hg * 4:hg * 4 + 4, :], S[:, hg * 4:hg * 4 + 4, :], r=["S"], w=["Sb"])
            pump(len(pending))
            for h in range(8):
                c.act(ph.junk[:n, 0:128], otok[:n, h, :], AF.Square, r=["otok"], w=["junk", "oss"], accum_out=gb2[:n, OSS + h:OSS + h + 1])
            c.act(gb2[:n, OSS:OSS + 8], gb2[:n, OSS:OSS + 8], AF.Sqrt, r=["oss"], w=["oss"], scale=1.0 / 128, bias=EPS)
            c.recip(gb2[:n, OSS:OSS + 8], gb2[:n, OSS:OSS + 8], r=["oss"], w=["oss"])
            c.tt("dve", otok[:n], otok[:n], bc(OSS), ALU.mult, r=["otok", "oss"], w=["otok"])
            c.tt("dve", otok[:n], otok[:n], g_on[:n, :].unsqueeze(1).to_broadcast([n, 8, 128]), ALU.mult, r=["otok", "g_on"], w=["otok"])
            c.tt("dve", og[:n, :], otok[:n].rearrange("p h d -> p (h d)"), zs[:n, :], ALU.mult, r=["otok", "tmpx"], w=["og"])
            pb = ph.bankb(7)
            for k in range(8):
                c.tr(pb[:, k * 128:k * 128 + n], og[:n, k * 128:(k + 1) * 128], ph.ident_b[:n, :n], r=["og", "ident_b"], w=["bank7"])
            c.cp("act", oT[:, :, :n], pb[:, :].rearrange("p (k t) -> p k t", t=128)[:, :, :n], r=["bank7"], w=["oT"])
            for hf in range(2):
                for k in range(8):
                    c.mm(ph.bank(hf)[:n, :], oT[:, k, :n], w_out[:, k, hf * 512:(hf + 1) * 512], start=(k == 0), stop=(k == 7), r=["oT"] + K_WOUT, w=["bank%d" % hf])
            ph.post_norm_res(0, n, g_post, "g_post", xs)
            ph.store_x(1, t, xs)
            if t.last:
                dst = O["p_delta_S"][t.b] if t.kind == "p" else O["s_delta_S"][t.s]
                c.dma("sp", dst.rearrange("h k v -> k h v"), S[:], r=["S"], w=["S_out"])
        ph.finish()

    def phase_ffn(L):
        phn = 2 if L == 0 else 4
        ph = Phase("f%d" % L)
        c = ph.c
        w_up = ph.sb("w_up", [128, 8, DFF2], BF16)
        w_dn = ph.sb("w_dn", [128, 22, D], BF16)
        c.load_w(w_up, I["f_w_up"][L], 8, DFF2, "w_up")
        c.load_w(w_dn, I["f_w_down"][L], 22, D, "w_dn")
        K_UP = c.load_w_keys("w_up", 8, DFF2)
        K_DN = c.load_w_keys("w_dn", 22, D)
        g_pre = ph.gain("g_pre", I["f_norm_pre"][L:L + 1, :])
        g_post = ph.gain("g_post", I["f_norm_post"][L:L + 1, :])
        cwrow = [ph.sb("cwrow%d" % j, [88, 128], F32) for j in range(2)]
        cw = ph.sb("cw", [128, 44, 4], F32)
        c.dma("sp", cwrow[0][:], I["f_conv_w"][L, 0:2, :].rearrange("j (c p) -> (j c) p", p=128), w=["cwrow0"])
        c.dma("sp", cwrow[1][0:44, :], I["f_conv_w"][L, 2:3, :].rearrange("j (c p) -> (j c) p", p=128), w=["cwrow1"])
        c.dma("sp", cwrow[1][44:88, :], I["f_conv_b"][L:L + 1, :].rearrange("j (c p) -> (j c) p", p=128), w=["cwrow1b"])
        for j in range(2):
            c.tr(ph.bank(0)[:, j * 88:(j + 1) * 88], cwrow[j][:88, :], ph.ident_f[:88, :88], r=["cwrow%d" % j, "cwrow1b", "ident_f"], w=["bank0"])
        c.cp("dve", cw[:], ph.bank(0)[:, 0:176].rearrange("p (j c) -> p c j", j=4), r=["bank0"], w=["cw"])
        ubuf = ph.sb("ubuf", [128, 44, 130], F32)
        yg = [[ph.sb("yg%d_%d" % (q, j), [128, 128], F32) for j in range(4)] for q in range(2)]
        yv = [[ph.sb("yv%d_%d" % (q, j), [128, 128], F32) for j in range(4)] for q in range(2)]
        mT = ph.sb("mT", [128, 22, 128], BF16)
        crow = ph.sb("crow", [16, 1024], F32)
        ubflat = ubuf[:, :, :].rearrange("p c t -> p (c t)")
        hs = ubflat[:, 0:1408].rearrange("p (b q) -> p b q", q=128)
        hist = ubflat[:, 1408:2816]
        histv = hist.rearrange("p (s j c) -> p c j s", s=16, j=2, c=44)
        UBK = ["ubg%d" % g for g in range(6)] + ["ubv%d" % g for g in range(6)]
        ftiles = [t for t in tiles if t.kind == "p"]
        if any(t.kind == "s" for t in tiles):
            ftiles.append(Tile("S", NS, 8192, True, True))
        GROUPS = [(k * 4, min(4, 22 - k * 4)) for k in range(6)]

        def fx_in(t):
            return xss[phn - 1][0:NS, :] if t.kind == "S" else x_in(phn, t)

        def fx_out(t):
            if t.kind == "S":
                return O["y_sample"][0:NS, :] if phn == 4 else xss[phn][0:NS, :]
            return x_out(phn, t)

        def st_norm(ti):
            t = ftiles[ti]
            xs = ti % 2
            c.dma("sp", ph.x[xs][:t.n, :], fx_in(t), w=["x%d" % xs])
            if t.first:
                if t.kind == "p":
                    c.memset("pool", ubuf[:, :, 0:2], 0.0, w=["ubg%d" % g for g in range(6)] + ["ubv%d" % g for g in range(6)])
                else:
                    c.dma("sp", hs, I["state_ffn_conv"][L].rearrange("s j (c p) -> (s j c) p", p=128).rearrange("(b r) p -> r b p", r=128), w=["hs"] + UBK)
                    for blk in range(11):
                        c.tr(ph.ps[:, blk * 128:(blk + 1) * 128], hs[:, blk, :], ph.ident_f[:, :], r=["hs", "ident_f"], w=["bank%d" % (blk // 4)])
                    c.cp("dve", hist, ph.ps[:, 0:1408], r=["bank0", "bank1", "bank2"], w=["hist"])
                    c.dma("sp", O["s_ffn_conv"][L, :, 0, :], I["state_ffn_conv"][L, :, 1, :], w=["sfc_out"])
            ph.norm_T(ph.x[xs][:t.n, :], "x%d" % xs, g_pre, "g_pre", t.n, 6)

        def st_up_conv(ti):
            t = ftiles[ti]
            n = t.n
            if t.last:
                M = min(n, 2) if t.kind == "p" else n
                for rd in range(6):
                    nh = 2 if rd < 5 else 1
                    for hf in range(nh):
                        for k in range(8):
                            c.mm(ph.bank(hf)[:M, :], ph.hT[:, k, n - M:n], w_up[:, k, rd * 1024 + hf * 512:rd * 1024 + (hf + 1) * 512], start=(k == 0), stop=(k == 7), r=["hT"] + K_UP, w=["bank%d" % hf])
                    c.cp("act", crow[:M, :nh * 512], ph.bank(0, 2)[:M, :nh * 512], r=["bank0", "bank1"], w=["crow"])
                    if t.kind == "p":
                        dst = O["p_ffn_conv"][L, t.b, 2 - M:2, rd * 1024:rd * 1024 + nh * 512]
                    else:
                        dst = O["s_ffn_conv"][L, :, 1, rd * 1024:rd * 1024 + nh * 512]
                    c.dma("sp", dst, crow[:M, :nh * 512], r=["crow"], w=["fc_out"])
            for gi, (c0, cnt) in enumerate(GROUPS):
                q = gi % 2
                bA, bB = 2 + 2 * q, 3 + 2 * q
                for (bk, base) in ((bA, c0), (bB, 22 + c0)):
                    for j in range(cnt):
                        ch = base + j
                        for k in range(8):
                            c.mm(ph.bank(bk)[:, j * 128:j * 128 + n], w_up[:, k, ch * 128:(ch + 1) * 128], ph.hT[:, k, :n], start=(k == 0), stop=(k == 7), r=["hT"] + K_UP, w=["bank%d" % bk])
                if t.kind == "p":
                    c.cp("act", ubuf[:, c0:c0 + cnt, 2:2 + n], ph.bank(bA).rearrange("p (j t) -> p j t", t=128)[:, 0:cnt, :n], r=["bank%d" % bA], w=["ubg%d" % gi])
                    c.cp("act", ubuf[:, 22 + c0:22 + c0 + cnt, 2:2 + n], ph.bank(bB).rearrange("p (j t) -> p j t", t=128)[:, 0:cnt, :n], r=["bank%d" % bB], w=["ubv%d" % gi])
                ys = [(yg[q][j], "yg%d_%d" % (q, j), c0 + j, bA, "ubg%d" % gi) for j in range(cnt)] + [(yv[q][j], "yv%d_%d" % (q, j), 22 + c0 + j, bB, "ubv%d" % gi) for j in range(cnt)]
                for (yb, yk, ch, bk, uk) in ys:
                    j = (ch - c0) % 22
                    c.act(yb[:, :n], ph.bank(bk)[:, j * 128:j * 128 + n], AF.Identity, r=["bank%d" % bk, "cw"], w=[yk], scale=cw[:, ch, 2:3], bias=cw[:, ch, 3:4])
                for tap in range(2):
                    for (yb, yk, ch, bk, uk) in ys:
                        if t.kind == "p":
                            src, sk = ubuf[:, ch, tap:tap + n], uk
                        else:
                            src, sk = histv[:, ch, tap, :n], "hist"
                        c.stt(yb[:, :n], src, cw[:, ch, tap:tap + 1], yb[:, :n], ALU.mult, ALU.add, r=[sk, "cw", yk], w=[yk])
                for j in range(cnt):
                    c.act(yg[q][j][:, :n], yg[q][j][:, :n], AF.Silu, r=["yg%d_%d" % (q, j)], w=["yg%d_%d" % (q, j)])
                for j in range(cnt):
                    c.tt("dve", mT[:, c0 + j, :n], yg[q][j][:, :n], yv[q][j][:, :n], ALU.mult, r=["yg%d_%d" % (q, j), "yv%d_%d" % (q, j)], w=["mT%d" % gi])
                if t.kind == "p" and not t.last:
                    c.cp("act", ubuf[:, c0:c0 + cnt, 0:2], ubuf[:, c0:c0 + cnt, n:n + 2], r=["ubg%d" % gi], w=["ubg%d" % gi])
                    c.cp("act", ubuf[:, 22 + c0:22 + c0 + cnt, 0:2], ubuf[:, 22 + c0:22 + c0 + cnt, n:n + 2], r=["ubv%d" % gi], w=["ubv%d" % gi])

        def st_down(ti):
            t = ftiles[ti]
            n = t.n
            xs = ti % 2
            for hf in range(2):
                for k in range(22):
                    c.mm(ph.bank(hf)[:n, :], mT[:, k, :n], w_dn[:, k, hf * 512:(hf + 1) * 512], start=(k == 0), stop=(k == 21), r=["mT%d" % (k // 4)] + K_DN, w=["bank%d" % hf])
            ph.post_norm_res(0, n, g_post, "g_post", xs)
            dst = fx_out(t)
            if dst is not None:
                c.dma("sp", dst, ph.x[xs][:n, :], r=["x%d" % xs], w=["xout"])

        st_norm(0)
        for ti in range(len(ftiles)):
            st_up_conv(ti)
            if ti + 1 < len(ftiles):
                st_norm(ti + 1)
            st_down(ti)
        ph.finish()


    def phase_mla(sample):
        ph = Phase("ms" if sample else "mp")
        c = ph.c
        kvwa = ph.sb("kvwa", [128, 8, 320], BF16)
        wqa = ph.sb("wqa", [128, 8, 384], BF16)
        wqb = ph.sb("wqb", [128, 3, 1536], BF16)
        bwo = ph.sb("bwo", [128, 8, D], BF16)
        wuv = ph.sb("wuv", [128, 2, 1024], BF16)
        wukr = ph.sb("wukr", [128, 2, 1024], BF16)
        wukT = ph.sb("wukT", [128, 8, 256], BF16)
        c.load_w(kvwa, I["kv_w_a"], 8, 320, "kvwa")
        c.load_w(wqa, I["b_w_q_a"], 8, 384, "wqa")
        c.load_w(wqb, I["b_w_q_b"], 3, 1536, "wqb")
        c.load_w(bwo, I["b_w_out"], 8, D, "bwo")
        c.load_w(wuv, I["kv_w_uv"], 2, 1024, "wuv")
        c.load_w(wukr, I["kv_w_uk"], 2, 1024, "wukr")
        K_KVWA, K_WQA, K_WQB = c.load_w_keys("kvwa", 8, 320), c.load_w_keys("wqa", 8, 384), c.load_w_keys("wqb", 3, 1536)
        K_BWO, K_WUV, K_WUKR = c.load_w_keys("bwo", 8, D), c.load_w_keys("wuv", 2, 1024), c.load_w_keys("wukr", 2, 1024)
        for rc in range(2):
            pb = ph.bankb(rc)
            for h in range(8):
                c.tr(pb[:, h * 128:(h + 1) * 128], wukr[:, rc, h * 128:(h + 1) * 128], ph.ident_b[:, :], r=K_WUKR + ["ident_b"], w=["bank%d" % rc])
            c.cp("dve", wukT[:, :, rc * 128:(rc + 1) * 128], pb[:, :].rearrange("p (h r) -> p h r", r=128), r=["bank%d" % rc], w=["wukT"])
        g_kv = ph.gain("g_kv", I["kv_norm"])
        g_kva = ph.gain("g_kva", I["kv_a_norm"])
        g_pre = ph.gain("g_pre", I["b_norm_pre"])
        g_post = ph.gain("g_post", I["b_norm_post"])
        g_qa = ph.gain("g_qa", I["b_q_a_norm"])
        negmask = ph.sb("negmask", [128, 128], F32)
        c.dma("sp", negmask[:], I["c_negmask"], w=["negmask"])
        cs = ph.sb("cs", [128, 64], F32)
        cf = ph.sb("cf", [128, 256], F32)
        cbf = ph.sb("cbf", [128, 256], BF16)
        kr = ph.sb("kr", [128, 64], F32)
        rt = ph.sb("rt", [128, 8, 32], F32)
        rt2 = ph.sb("rt2", [128, 8, 32], F32)
        kr2b = ph.sb("kr2b", [128, 128], BF16)
        qan = ph.sb("qan", [128, 384], BF16)
        qaT = ph.sb("qaT", [128, 3, 128], BF16)
        qnopeT = ph.sb("qnopeT", [128, 8, 128], BF16)
        qlatT = ph.sb("qlatT", [128, 16, 128], BF16)
        qpe = ph.sb("qpe", [128, 8, 64], F32)
        qpeb = ph.sb("qpeb", [128, 8, 64], BF16)
        ob = ph.sb("ob", [128, 8, 128], BF16)
        oT = ph.sb("oT", [128, 8, 128], BF16)
        olT = ph.sb("olT", [128, 2, 128], BF16)
        if not sample:
            cT_all = ph.sb("cT_all", [128, 2, LP], BF16)
            krT_all = ph.sb("krT_all", [128, LP], BF16)
            ctok_all = ph.sb("ctok_all", [128, 17, 256], BF16)
            qpeT = ph.sb("qpeT", [128, 4, 128], BF16)
            pbuf = [ph.sb("pbuf%d" % j, [128, LP], BF16) for j in range(2)]
            pT = [ph.sb("pT%d" % j, [128, 17, 128], BF16) for j in range(2)]
            sdg = [ph.sb("sdg%d" % j, [128, 128], F32) for j in range(2)]
            olT2 = [ph.sb("olT2_%d" % j, [128, 2, 128], BF16) for j in range(2)]
            sth = ph.sb("sth", [128, 2, 8], F32)
        else:
            cTn = ph.sb("cTn", [128, 2, 2], BF16)
            krTn = ph.sb("krTn", [128, 2], BF16)
            ptb = ph.sb("ptb", [128, NS * NPG], I32)
            idx_all = ph.sb("idx_all", [128, NS * NPG], I32)
            iota = ph.sb("iota", [128, 1], F32)
            c.dma("sp", iota[:], I["c_iota"], w=["iota"])
            c.dma("sp", ptb[:], I["page_table"].rearrange("s j -> (s j)").partition_broadcast(128), w=["ptb"])
            c.ts("dve", idx_all[:], ptb[:], 128.0, iota[:, 0:1], ALU.mult, ALU.add, r=["ptb", "iota"], w=["idx"])
            cpg = ph.sb("cpg", [128, NPG, 256], BF16)
            rpg = ph.sb("rpg", [128, NPG, 64], BF16)
            cTg = [ph.sb("cTg%d" % j, [128, 4, 2, 128], BF16) for j in range(2)]
            krTg = [ph.sb("krTg%d" % j, [64, 4, 128], BF16) for j in range(2)]
            s_sb = ph.sb("s_sb", [8, 8200], F32)
            p_sb = ph.sb("p_sb", [8, 8200], BF16)
            qls = ph.sb("qls", [128, 2, 8], BF16)
            qps = ph.sb("qps", [64, 8], BF16)
            pTs = ph.sb("pTs", [128, NPG, 8], BF16)
            pself = ph.sb("pself", [1, 8], BF16)
            olTs = ph.sb("olTs", [128, 2, 8], BF16)
            st8 = ph.sb("st8", [8, 8], F32)
            rlrow = ph.sb("rlrow", [1, 8], F32)
        flat_c = I["cache_kv_latent"]
        flat_r = I["cache_k_rope"]

        def rope(dst, src, n, H, rk, wk):
            cosb = cs[:n, 0:32].unsqueeze(1).to_broadcast([n, H, 32])
            sinb = cs[:n, 32:64].unsqueeze(1).to_broadcast([n, H, 32])
            x1, x2 = src[:, :, 0:32], src[:, :, 32:64]
            c.tt("dve", rt[:n, :H, :], x1, cosb, ALU.mult, r=rk + ["cs"], w=["rt"])
            c.tt("dve", rt2[:n, :H, :], x2, sinb, ALU.mult, r=rk + ["cs"], w=["rt2"])
            c.tt("dve", dst[:, :, 0:32], rt[:n, :H, :], rt2[:n, :H, :], ALU.subtract, r=["rt", "rt2"], w=wk)
            c.tt("dve", rt[:n, :H, :], x2, cosb, ALU.mult, r=rk + ["cs"], w=["rt"])
            c.tt("dve", rt2[:n, :H, :], x1, sinb, ALU.mult, r=rk + ["cs"], w=["rt2"])
            c.tt("dve", dst[:, :, 32:64], rt[:n, :H, :], rt2[:n, :H, :], ALU.add, r=["rt", "rt2"], w=wk)

        my_tiles = [t for t in tiles if (t.kind == "s") == sample]
        for ti, t in enumerate(my_tiles):
            n = t.n
            xs = ti % 2
            ph.load_x(3, t, xs)
            xk = "x%d" % xs
            x = ph.x[xs]
            rrow = 2064 if sample else t.pos
            c.dma("sp", cs[:n, :], I["c_rope"][rrow:rrow + n, :], w=["cs"])
            ph.norm_T(x[:n, :], xk, g_kv, "g_kv", n, 1)
            for k in range(8):
                c.mm(ph.bank(2)[:n, 0:320], ph.hT[:, k, :n], kvwa[:, k, :], start=(k == 0), stop=(k == 7), r=["hT"] + K_KVWA, w=["bank2"])
            ph.rstd(ph.bank(2)[:n, 0:256], n, 256, 2, ["bank2"])
            c.stt(cf[:n, :], ph.bank(2)[:n, 0:256], ph.st[:n, 2:3], g_kva[:n, :], ALU.mult, ALU.mult, r=["bank2", "st2", "g_kva"], w=["cf"])
            c.dma("sp", O["s_kv_latent"][t.s:t.s + 1, :] if sample else O["p_kv_latent"][t.b, t.pos:t.pos + n, :], cf[:n, :], r=["cf"], w=["kv_out"])
            cb = cbf[:n, :] if sample else ctok_all[:n, t.i, :]
            cbk = "cbf" if sample else "ctok_all"
            c.cp("act", cb, cf[:n, :], r=["cf"], w=[cbk])
            pb = ph.bankb(3)
            for rc in range(2):
                c.tr(pb[:, rc * 128:rc * 128 + n], cb[:, rc * 128:(rc + 1) * 128], ph.ident_b[:n, :n], r=[cbk, "ident_b"], w=["bank3"])
            rope(kr[:n, :].rearrange("p (h e) -> p h e", h=1), ph.bank(2)[:n, 256:320].rearrange("p (h e) -> p h e", h=1), n, 1, ["bank2"], ["kr"])
            c.dma("sp", O["s_k_rope"][t.s:t.s + 1, :] if sample else O["p_k_rope"][t.b, t.pos:t.pos + n, :], kr[:n, :], r=["kr"], w=["kr_out"])
            c.cp("act", kr2b[:n, 0:64], kr[:n, :], r=["kr"], w=["kr2b"])
            c.cp("act", kr2b[:n, 64:128], kr[:n, :], r=["kr"], w=["kr2b"])
            c.tr(pb[:, 256:256 + n], kr2b[:n, :], ph.ident_b[:n, :n], r=["kr2b", "ident_b"], w=["bank3"])
            if sample:
                c.cp("dve", cTn[:, :, 0:1], pb[:, 0:256].rearrange("p (r t) -> p r t", t=128)[:, :, 0:1], r=["bank3"], w=["cTn"])
                c.cp("dve", krTn[:, 0:1], pb[:, 256:257], r=["bank3"], w=["krTn"])
            else:
                c.cp("dve", cT_all[:, :, t.pos:t.pos + n], pb[:, 0:256].rearrange("p (r t) -> p r t", t=128)[:, :, :n], r=["bank3"], w=["cT_all"])
                c.cp("dve", krT_all[:, t.pos:t.pos + n], pb[:, 256:256 + n], r=["bank3"], w=["krT_all"])
            ph.norm_T(x[:n, :], xk, g_pre, "g_pre", n, 1)
            for k in range(8):
                c.mm(ph.bank(2)[:n, 0:384], ph.hT[:, k, :n], wqa[:, k, :], start=(k == 0), stop=(k == 7), r=["hT"] + K_WQA, w=["bank2"])
            ph.norm_T(ph.bank(2)[:n, 0:384], "bank2", g_qa, "g_qa", n, 4, width=384, hn=qan, hT=qaT, hkey="qaT", col=3)
            for h in range(8):
                bk = 5 + h // 4
                for k in range(3):
                    c.mm(ph.bank(bk)[:, (h % 4) * 128:(h % 4) * 128 + n], wqb[:, k, h * 192:h * 192 + 128], qaT[:, k, :n], start=(k == 0), stop=(k == 2), r=["qaT"] + K_WQB, w=["bank%d" % bk])
            for g in range(2):
                c.cp("act", qnopeT[:, g * 4:(g + 1) * 4, :n], ph.bank(5 + g).rearrange("p (j t) -> p j t", t=128)[:, :, :n], r=["bank%d" % (5 + g)], w=["qnopeT"])
            for h in range(8):
                for rc in range(2):
                    q = h * 2 + rc
                    bk = 2 + q // 4
                    c.mm(ph.bank(bk)[:, (q % 4) * 128:(q % 4) * 128 + n], wukT[:, h, rc * 128:(rc + 1) * 128], qnopeT[:, h, :n], r=["wukT", "qnopeT"], w=["bank%d" % bk])
            for g in range(4):
                c.act(qlatT[:, g * 4:(g + 1) * 4, :n], ph.bank(2 + g).rearrange("p (j t) -> p j t", t=128)[:, :, :n], AF.Copy, r=["bank%d" % (2 + g)], w=["qlatT"], scale=MLA_SCALE)
            for k in range(3):
                c.mm(ph.bank(6)[:n, :], qaT[:, k, :n], wqb[:, k, :].rearrange("p (h e) -> p h e", e=192)[:, :, 128:192], start=(k == 0), stop=(k == 2), r=["qaT"] + K_WQB, w=["bank6"])
            rope(qpe[:n], ph.bank(6)[:n, :].rearrange("p (h e) -> p h e", e=64), n, 8, ["bank6"], ["qpe"])
            c.act(qpeb[:n], qpe[:n], AF.Copy, r=["qpe"], w=["qpeb"], scale=MLA_SCALE)
            if not sample:
                pb7 = ph.bankb(7)
                for j in range(4):
                    c.tr(pb7[:, j * 128:j * 128 + n], qpeb[:n, 2 * j:2 * j + 2, :].rearrange("p h e -> p (h e)"), ph.ident_b[:n, :n], r=["qpeb", "ident_b"], w=["bank7"])
                c.cp("dve", qpeT[:, :, :n], pb7[:, 0:512].rearrange("p (j t) -> p j t", t=128)[:, :, :n], r=["bank7"], w=["qpeT"])
                K_ = t.pos + n
                nb = (K_ + 511) // 512
                nkt = t.i + 1
                piped = nb <= 4

                def region(h):
                    if piped:
                        rb = 4 * (h % 2)
                        return rb, rb, rb + 3
                    return 0, 5, 0

                def st_S(h):
                    sbk = region(h)[0]
                    hb = 64 * (h % 2)
                    for kb in range(nb):
                        k0, k1 = kb * 512, min(K_, kb * 512 + 512)
                        bk = sbk + kb
                        bkk = "bank%d" % bk
                        c.mm(ph.bank(bk)[:n, 0:k1 - k0], qlatT[:, 2 * h, :n], cT_all[:, 0, k0:k1], start=True, stop=False, r=["qlatT", "cT_all"], w=[bkk])
                        c.mm(ph.bank(bk)[:n, 0:k1 - k0], qlatT[:, 2 * h + 1, :n], cT_all[:, 1, k0:k1], start=False, stop=False, r=["qlatT", "cT_all"], w=[bkk])
                        c.mm(ph.bank(bk)[:n, 0:k1 - k0], qpeT[hb:hb + 64, h // 2, :n], krT_all[hb:hb + 64, k0:k1], start=False, stop=True, r=["qpeT", "krT_all"], w=[bkk])

                def st_X(h):
                    q = h % 2
                    sbk = region(h)[0]
                    sck = ["bank%d" % (sbk + kb) for kb in range(nb)]
                    sc = ph.ps[:n, sbk * 512:sbk * 512 + K_]
                    sk = lambda j: "sth%d_%d" % (q, j)
                    sv = lambda j: sth[:n, q, j:j + 1]
                    c.red(sv(0), sc, ALU.max, r=sck, w=[sk(0)])
                    c.ts("dve", sv(1), sv(0), -1.0, None, ALU.mult, r=[sk(0)], w=[sk(1)])
                    c.tt("dve", sdg[q][:n, :n], sc[:, t.pos:K_], negmask[:n, :n], ALU.add, r=sck + ["negmask"], w=["sdg%d" % q])
                    if t.pos > 0:
                        c.act(pbuf[q][:n, 0:t.pos], sc[:, 0:t.pos], AF.Exp, r=sck + [sk(1)], w=["pbuf%d" % q, sk(2)], bias=sv(1), accum_out=sv(2))
                    c.act(pbuf[q][:n, t.pos:K_], sdg[q][:n, :n], AF.Exp, r=["sdg%d" % q, sk(1)], w=["pbuf%d" % q, sk(3)], bias=sv(1), accum_out=sv(3))
                    if t.pos > 0:
                        c.tt("dve", sv(3), sv(3), sv(2), ALU.add, r=[sk(2), sk(3)], w=[sk(3)])
                    c.recip(sv(4), sv(3), r=[sk(3)], w=[sk(4)])

                def st_T(h):
                    q = h % 2
                    tbk = region(h)[1]
                    for kt in range(nkt):
                        k0 = 0 if kt == 0 else 16 + 128 * (kt - 1)
                        nk = 16 if kt == 0 else 128
                        bk = tbk + kt // 8
                        c.tr(ph.bankb(bk)[:nk, (kt % 8) * 128:(kt % 8) * 128 + n], pbuf[q][:n, k0:k0 + nk], ph.ident_b[:n, :n], r=["pbuf%d" % q, "ident_b"], w=["bank%d" % bk])
                    for j in range((nkt + 7) // 8):
                        m = min(8, nkt - j * 8)
                        c.cp("act" if j % 2 == 0 else "dve", pT[q][:, j * 8:j * 8 + m, :n], ph.bankb(tbk + j)[:, 0:m * 128].rearrange("p (j t) -> p j t", t=128)[:, :, :n], r=["bank%d" % (tbk + j)], w=["pT%d" % q])

                def st_V(h):
                    q = h % 2
                    vbk = region(h)[2]
                    vk = "bank%d" % vbk
                    for rc in range(2):
                        for kt in range(nkt):
                            nk = 16 if kt == 0 else 128
                            c.mm(ph.bank(vbk)[:, rc * 128:rc * 128 + n], ctok_all[:nk, kt, rc * 128:(rc + 1) * 128], pT[q][:nk, kt, :n], start=(kt == 0), stop=(kt == nkt - 1), r=["ctok_all", "pT%d" % q], w=[vk])
                    c.cp("dve", olT2[q][:, :, :n], ph.bank(vbk)[:, 0:256].rearrange("p (r t) -> p r t", t=128)[:, :, :n], r=[vk], w=["olT%d" % q])
                    for rc in range(2):
                        c.mm(ph.bank(vbk)[:n, 256:384], olT2[q][:, rc, :n], wuv[:, rc, h * 128:(h + 1) * 128], start=(rc == 0), stop=(rc == 1), r=["olT%d" % q] + K_WUV, w=[vk])
                    c.act(ob[:n, h, :], ph.bank(vbk)[:n, 256:384], AF.Identity, r=[vk, "sth%d_4" % q], w=["ob"], scale=sth[:n, q, 4:5])

                if piped:
                    st_S(0)
                    for h in range(8):
                        if h < 7:
                            st_S(h + 1)
                        st_X(h)
                        st_T(h)
                        st_V(h)
                else:
                    for h in range(8):
                        st_S(h)
                        st_X(h)
                        st_T(h)
                        st_V(h)
            else:
                c.cp("dve", qls[:, :, :], qlatT[:, :, 0:1].rearrange("p (h r) t -> p r (h t)", r=2), r=["qlatT"], w=["qls"])
                pb7 = ph.bankb(7)
                for h in range(8):
                    c.tr(pb7[:64, 2 * h:2 * h + 1], qpeb[:1, h, :], ph.ident_b[:1, :1], r=["qpeb", "ident_b"], w=["bank7"])
                c.cp("dve", qps[:, :], pb7[:64, 0:16].rearrange("p (h two) -> p h two", two=2)[:, :, 0], r=["bank7"], w=["qps"])
                idx = idx_all[:, t.s * NPG:(t.s + 1) * NPG]
                for j in range(NPG):
                    self_ = c
                    c.P.op("pool", lambda e, j=j, idx=idx: e.indirect_dma_start(out=cpg[:, j, :], out_offset=None, in_=flat_c, in_offset=bass.IndirectOffsetOnAxis(ap=idx[:, j:j + 1], axis=0)), ["idx"], ["cpg%d_%d" % (j // 4, j % 4)], dma=True)
                    c.P.op("pool", lambda e, j=j, idx=idx: e.indirect_dma_start(out=rpg[:, j, :], out_offset=None, in_=flat_r, in_offset=bass.IndirectOffsetOnAxis(ap=idx[:, j:j + 1], axis=0)), ["idx"], ["rpg%d_%d" % (j // 4, j % 4)], dma=True)
                for g in range(16):
                    A, B, C = g % 2, 2 + g % 2, 4 + g % 2
                    pa, pbb = ph.bankb(A), ph.bankb(B)
                    for j in range(4):
                        for rc in range(2):
                            c.tr(pa[:, (j * 2 + rc) * 128:(j * 2 + rc + 1) * 128], cpg[:, 4 * g + j, rc * 128:(rc + 1) * 128], ph.ident_b[:, :], r=["cpg%d_%d" % (g, j), "ident_b"], w=["bank%d" % A])
                        c.tr(pbb[:64, j * 128:(j + 1) * 128], rpg[:, 4 * g + j, :], ph.ident_b[:, :], r=["rpg%d_%d" % (g, j), "ident_b"], w=["bank%d" % B])
                    c.cp("act", cTg[g % 2][:], pa[:, :].rearrange("p (j r t) -> p j r t", r=2, t=128), r=["bank%d" % A], w=["cTg%d" % (g % 2)])
                    c.cp("dve", krTg[g % 2][:], pbb[:64, 0:512].rearrange("p (j t) -> p j t", t=128), r=["bank%d" % B], w=["krTg%d" % (g % 2)])
                    for rc in range(2):
                        c.mm(ph.bank(C)[:8, :], qls[:, rc, :], cTg[g % 2][:, :, rc, :], start=(rc == 0), stop=False, r=["qls", "cTg%d" % (g % 2)], w=["bank%d" % C])
                    c.mm(ph.bank(C)[:8, :], qps[:64, :], krTg[g % 2][:64, :, :], start=False, stop=True, r=["qps", "krTg%d" % (g % 2)], w=["bank%d" % C])
                    c.cp("act", s_sb[:8, g * 512:(g + 1) * 512], ph.bank(C)[:8, :], r=["bank%d" % C], w=["s_sb"])
                for rc in range(2):
                    c.mm(ph.bank(6)[:8, 0:1], qls[:, rc, :], cTn[:, rc, 0:1], start=(rc == 0), stop=False, r=["qls", "cTn"], w=["bank6"])
                c.mm(ph.bank(6)[:8, 0:1], qps[:64, :], krTn[:64, 0:1], start=False, stop=True, r=["qps", "krTn"], w=["bank6"])
                c.cp("act", s_sb[:8, 8192:8193], ph.bank(6)[:8, 0:1], r=["bank6"], w=["s_sb"])
                c.red(st8[:8, 0:1], s_sb[:8, 0:8193], ALU.max, r=["s_sb"], w=["st8a"])
                c.ts("dve", st8[:8, 1:2], st8[:8, 0:1], -1.0, None, ALU.mult, r=["st8a"], w=["st8b"])
                c.act(p_sb[:8, 0:8193], s_sb[:8, 0:8193], AF.Exp, r=["s_sb", "st8b"], w=["p_sb", "st8c"], bias=st8[:8, 1:2], accum_out=st8[:8, 2:3])
                c.recip(st8[:8, 3:4], st8[:8, 2:3], r=["st8c"], w=["st8d"])
                pb6 = ph.bankb(6)
                for j in range(NPG):
                    c.tr(pb6[:, j * 8:j * 8 + 8], p_sb[:8, j * 128:(j + 1) * 128], ph.ident_b[:8, :8], r=["p_sb", "ident_b"], w=["bank6"])
                c.cp("act", pTs[:], pb6[:, 0:512].rearrange("p (j h) -> p j h", h=8), r=["bank6"], w=["pTs"])
                c.tr(pb7[:1, 0:8], p_sb[:8, 8192:8193], ph.ident_b[:8, :8], r=["p_sb", "ident_b"], w=["bank7"])
                c.cp("dve", pself[:1, :], pb7[:1, 0:8], r=["bank7"], w=["pself"])
                for rc in range(2):
                    for j in range(NPG):
                        c.mm(ph.bank(0)[:, rc * 8:rc * 8 + 8], cpg[:, j, rc * 128:(rc + 1) * 128], pTs[:, j, :], start=(j == 0), stop=False, r=["cpg%d_%d" % (j // 4, j % 4), "pTs"], w=["bank0"])
                    c.mm(ph.bank(0)[:, rc * 8:rc * 8 + 8], cbf[0:1, rc * 128:(rc + 1) * 128], pself[0:1, :], start=False, stop=True, r=["cbf", "pself"], w=["bank0"])
                c.cp("dve", olTs[:], ph.bank(0)[:, 0:16].rearrange("p (r h) -> p r h", h=8), r=["bank0"], w=["olTs"])
                for h in range(8):
                    for rc in range(2):
                        c.mm(ph.bank(1 + h // 4)[:1, (h % 4) * 128:(h % 4 + 1) * 128], olTs[:, rc, h:h + 1], wuv[:, rc, h * 128:(h + 1) * 128], start=(rc == 0), stop=(rc == 1), r=["olTs"] + K_WUV, w=["bank%d" % (1 + h // 4)])
                c.tr(ph.bank(3)[:1, 0:8], st8[:8, 3:4], ph.ident_f[:8, :8], r=["st8d", "ident_f"], w=["bank3"])
                c.cp("dve", rlrow[:1, :], ph.bank(3)[:1, 0:8], r=["bank3"], w=["rlrow"])
                c.tt("dve", ob[:1], ph.bank(1, 2)[:1, :].rearrange("p (h d) -> p h d", d=128), rlrow[:1, :].unsqueeze(2).to_broadcast([1, 8, 128]), ALU.mult, r=["bank1", "bank2", "rlrow"], w=["ob"])
            pb5 = ph.bankb(5)
            for h in range(8):
                c.tr(pb5[:, h * 128:h * 128 + n], ob[:n, h, :], ph.ident_b[:n, :n], r=["ob", "ident_b"], w=["bank5"])
            c.cp("act", oT[:, :, :n], pb5[:, :].rearrange("p (k t) -> p k t", t=128)[:, :, :n], r=["bank5"], w=["oT"])
            for hf in range(2):
                for k in range(8):
                    c.mm(ph.bank(hf)[:n, :], oT[:, k, :n], bwo[:, k, hf * 512:(hf + 1) * 512], start=(k == 0), stop=(k == 7), r=["oT"] + K_BWO, w=["bank%d" % hf])
            ph.post_norm_res(0, n, g_post, "g_post", xs)
            ph.store_x(3, t, xs)
        ph.finish()

    PH = {1: phase_gdn, 2: lambda: phase_ffn(0), 3: lambda: (phase_mla(False), phase_mla(True)), 31: lambda: phase_mla(False), 32: lambda: phase_mla(True), 4: lambda: phase_ffn(1)}
    for p_ in phases:
        PH[p_]()
    ges.close()
    return nc


def _consts():
    i = np.arange(128)
    c = {}
    c["c_ident"] = np.eye(128, dtype=np.float32)
    c["c_triu"] = (i[:, None] <= i[None, :]).astype(np.float32)
    c["c_lstrict"] = (i[:, None] > i[None, :]).astype(np.float32)
    c["c_lincl"] = (i[:, None] >= i[None, :]).astype(np.float32)
    c["c_negmask"] = np.where(i[None, :] <= i[:, None], 0.0, NEG).astype(np.float32)
    pos = np.concatenate([np.arange(LP), [8192]]).astype(np.float32)
    inv = (10000.0 ** (-np.arange(32, dtype=np.float32) / 32)).astype(np.float32)
    ang = (pos[:, None] * inv[None, :]).astype(np.float32)
    c["c_rope"] = np.concatenate([np.cos(ang), np.sin(ang)], axis=1).astype(np.float32)
    c["c_iota"] = i.astype(np.float32).reshape(128, 1)
    return c


def make_in_maps(inp):
    f = lambda a: np.ascontiguousarray(a)
    shared = {
        "cache_kv_latent": f(inp["cache_kv_latent"]).reshape(-1, 256),
        "cache_k_rope": f(inp["cache_k_rope"]).reshape(-1, 64),
        "meta_tokens": f(inp["meta_tokens"]),
        "a_norm_pre": f(inp["a_norm_pre"]), "a_norm_post": f(inp["a_norm_post"]), "a_w_in": f(inp["a_w_in"][0]),
        "a_conv_w": f(inp["a_conv_w"][0]), "a_log": f(inp["a_log"]), "a_dt_bias": f(inp["a_dt_bias"]),
        "a_out_norm": f(inp["a_out_norm"]), "a_w_out": f(inp["a_w_out"][0]),
        "kv_norm": f(inp["kv_norm"]).reshape(1, -1), "kv_w_a": f(inp["kv_w_a"]), "kv_a_norm": f(inp["kv_a_norm"]).reshape(1, -1),
        "kv_w_uk": f(inp["kv_w_uk"]).reshape(256, 1024), "kv_w_uv": f(inp["kv_w_uv"]).reshape(256, 1024),
        "b_norm_pre": f(inp["b_norm_pre"]), "b_norm_post": f(inp["b_norm_post"]), "b_w_q_a": f(inp["b_w_q_a"][0]),
        "b_q_a_norm": f(inp["b_q_a_norm"]), "b_w_q_b": f(inp["b_w_q_b"][0]), "b_w_out": f(inp["b_w_out"][0]),
        "f_norm_pre": f(inp["f_norm_pre"]), "f_norm_post": f(inp["f_norm_post"]), "f_w_up": f(inp["f_w_up"]),
        "f_conv_w": f(inp["f_conv_w"]), "f_conv_b": f(inp["f_conv_b"]), "f_w_down": f(inp["f_w_down"]),
    }
    shared.update(_consts())
    maps = []
    for cidx in range(8):
        m = dict(shared)
        m["x_prompt"] = f(inp["x_prompt"][NB * cidx:NB * (cidx + 1)])
        sl = slice(NS * cidx, NS * (cidx + 1))
        m["x_sample"] = f(inp["x_sample"][sl, 0])
        m["state_delta_S"] = f(inp["state_delta_S"][0, sl])
        m["state_delta_conv"] = f(inp["state_delta_conv"][0, sl])
        m["state_ffn_conv"] = f(inp["state_ffn_conv"][:, sl])
        m["page_table"] = f(inp["page_table"][sl]).astype(np.int32)
        maps.append(m)
    return maps


def assemble(res):
    cat = lambda k, ax=0: np.concatenate([r[k] for r in res], axis=ax)
    return (
        cat("y_prompt"), cat("y_sample")[:, None, :], cat("p_delta_S")[None], cat("p_delta_conv")[None],
        cat("p_ffn_conv", 1), cat("p_kv_latent"), cat("p_k_rope"), cat("s_delta_S")[None], cat("s_delta_conv")[None],
        cat("s_ffn_conv", 1), cat("s_kv_latent")[:, None, :], cat("s_k_rope")[:, None, :],
    )


def kernel(**inputs):
    inp = {k: np.asarray(v) for k, v in inputs.items()}
    nc = build_program()
    res = run_bass_kernel_spmd(nc, make_in_maps(inp), core_ids=list(range(8)))
    return tuple(np.ascontiguousarray(a.astype(np.float32)) for a in assemble(res.results))
```

```python
import math
from contextlib import ExitStack

import numpy as np
import concourse.bass as bass
import concourse.mybir as mybir
from concourse.bass_utils import run_bass_kernel_spmd

F32 = mybir.dt.float32
BF16 = mybir.dt.bfloat16
I32 = mybir.dt.int32
AF = mybir.ActivationFunctionType
ALU = mybir.AluOpType
AX = mybir.AxisListType

ENGS = ("pe", "act", "dve", "pool", "sp")
STRICT = True
NDSEM = {"pe": 1, "act": 1, "dve": 1, "pool": 24, "sp": 40}


class SemState:
    def __init__(self, nc, es):
        self.esem = {e: es.enter_context(nc.semaphore("sem_" + e)) for e in ENGS}
        self.dsem = {}
        for e in ("pool", "sp"):
            for j in range(NDSEM[e]):
                self.dsem[(e, j)] = es.enter_context(nc.semaphore("dsem_%s_%d" % (e, j)))
        self.ecnt = {e: 0 for e in ENGS}
        self.dma_n = {e: 0 for e in ENGS}
        self.barrier = 0


class Prog:
    def __init__(self, nc, G, strict_same_engine=STRICT):
        self.nc = nc
        self.G = G
        self.strict = strict_same_engine
        self.ops = []
        self.per_eng = {e: [] for e in ENGS}
        self.res = {}
        self.dma_n = dict(G.dma_n)

    def op(self, eng, fn, r=(), w=(), dma=None):
        oid = len(self.ops)
        deps = set()
        for k in r:
            st = self.res.get(k)
            if st is not None and st[0] is not None:
                deps.add(st[0])
        for k in w:
            st = self.res.get(k)
            if st is not None:
                if st[0] is not None:
                    deps.add(st[0])
                deps.update(st[1].values())
                deps.update(st[2])
        for k in r:
            st = self.res.get(k)
            if st is None:
                st = self.res[k] = [None, {}, []]
            if dma is None:
                st[1][eng] = oid
            else:
                st[2].append(oid)
        for k in w:
            self.res[k] = [oid, {}, []]
        keep = []
        for d in deps:
            p = self.ops[d]
            if p["dma"] is None and p["eng"] == eng and (eng == "pe" or not self.strict):
                continue
            keep.append(d)
        val = None
        if dma is not None:
            i = self.dma_n[eng]
            self.dma_n[eng] = i + 1
            dma = (eng, i % NDSEM[eng])
            val = 16 * (i // NDSEM[eng] + 1)
        self.ops.append(dict(eng=eng, fn=fn, deps=keep, dma=dma, dval=val, sig=False, cnt=None))
        self.per_eng[eng].append(oid)
        return oid

    def emit(self):
        nc = self.nc
        ops = self.ops
        G = self.G
        for o in ops:
            for d in o["deps"]:
                ops[d]["sig"] = True
        final = {}
        for e in ENGS:
            if e != "sp":
                comp = [oid for oid in self.per_eng[e] if ops[oid]["dma"] is None]
                if comp:
                    ops[comp[-1]]["sig"] = True
            c = G.ecnt[e]
            for oid in self.per_eng[e]:
                o = ops[oid]
                if o["dma"] is None and o["sig"]:
                    c += 1
                    o["cnt"] = c
            final[e] = c
        final["sp"] += 1
        dtot = {}
        for e in ("pool", "sp"):
            for j in range(NDSEM[e]):
                if self.dma_n[e] > j:
                    dtot[(e, j)] = 16 * ((self.dma_n[e] - 1 - j) // NDSEM[e] + 1)
        esem, dsem = G.esem, G.dsem
        with nc.Block() as block:

            def run(e, eng):
                seen = {}
                if G.barrier > 0:
                    eng.wait_ge(esem["sp"], G.barrier)
                for oid in self.per_eng[e]:
                    o = ops[oid]
                    need = {}
                    for d in o["deps"]:
                        p = ops[d]
                        if p["dma"] is not None:
                            key, v = ("d", p["dma"]), p["dval"]
                        else:
                            key, v = ("e", p["eng"]), p["cnt"]
                        if v > need.get(key, 0):
                            need[key] = v
                    for key, v in need.items():
                        if seen.get(key, 0) >= v:
                            continue
                        seen[key] = v
                        eng.wait_ge(dsem[key[1]] if key[0] == "d" else esem[key[1]], v)
                    if o["dma"] is not None and o["dval"] > 16:
                        key = ("d", o["dma"])
                        if seen.get(key, 0) < o["dval"] - 16:
                            seen[key] = o["dval"] - 16
                            eng.wait_ge(dsem[o["dma"]], o["dval"] - 16)
                    ins = o["fn"](eng)
                    if o["dma"] is not None:
                        ins.then_inc(dsem[o["dma"]], 16)
                    elif o["sig"]:
                        ins.then_inc(esem[e], 1)
                if e == "sp":
                    for k, tot in dtot.items():
                        eng.wait_ge(dsem[k], tot)
                    for e2 in ENGS:
                        if e2 != "sp" and final[e2] > 0:
                            eng.wait_ge(esem[e2], final[e2])
                    eng.nop().then_inc(esem["sp"], 1)

            @block.tensor
            def _(eng):
                run("pe", eng)

            @block.scalar
            def _(eng):
                run("act", eng)

            @block.vector
            def _(eng):
                run("dve", eng)

            @block.gpsimd
            def _(eng):
                run("pool", eng)

            @block.sync
            def _(eng):
                run("sp", eng)
        if DEBUG_SCRATCH:
            print('phase ops', {e: len(self.per_eng[e]) for e in ENGS}, 'sig', final, flush=True)
        G.ecnt = final
        G.dma_n = dict(self.dma_n)
        G.barrier = final["sp"]


D = 1024
NQKV = 3072
GIN = 4112
DFF = 2816
DFF2 = 5632
LP = 2064
NB = 2
NS = 16
NPG = 64
EPS = 1e-6
MLA_SCALE = 1.0 / math.sqrt(192.0)
NEG = -1.0e30
DEBUG_SCRATCH = False
HACK_SRC = {}


class Ctx:
    def __init__(self, nc, P):
        self.nc = nc
        self.P = P

    def mm(self, out, lhsT, rhs, start=True, stop=True, r=(), w=()):
        self.P.op("pe", lambda e, a=(out, lhsT, rhs, start, stop): e.matmul(a[0], lhsT=a[1], rhs=a[2], start=a[3], stop=a[4]), r, w)

    def tr(self, out, in_, ident, r=(), w=()):
        self.P.op("pe", lambda e, a=(out, in_, ident): e.transpose(a[0], a[1], a[2]), r, w)

    def act(self, out, in_, func, r=(), w=(), **kw):
        self.P.op("act", lambda e, a=(out, in_, func, kw): e.activation(out=a[0], in_=a[1], func=a[2], **a[3]), r, w)

    def ts(self, eng, out, in0, s1, s2, op0, op1=None, r=(), w=()):
        if op1 is None:
            self.P.op(eng, lambda e, a=(out, in0, s1, op0): e.tensor_scalar(out=a[0], in0=a[1], scalar1=a[2], scalar2=None, op0=a[3]), r, w)
        else:
            self.P.op(eng, lambda e, a=(out, in0, s1, s2, op0, op1): e.tensor_scalar(out=a[0], in0=a[1], scalar1=a[2], scalar2=a[3], op0=a[4], op1=a[5]), r, w)

    def tt(self, eng, out, in0, in1, op, r=(), w=()):
        self.P.op(eng, lambda e, a=(out, in0, in1, op): e.tensor_tensor(out=a[0], in0=a[1], in1=a[2], op=a[3]), r, w)

    def stt(self, out, in0, scalar, in1, op0, op1, r=(), w=()):
        self.P.op("dve", lambda e, a=(out, in0, scalar, in1, op0, op1): e.scalar_tensor_tensor(out=a[0], in0=a[1], scalar=a[2], in1=a[3], op0=a[4], op1=a[5]), r, w)

    def cp(self, eng, out, in_, r=(), w=()):
        if eng == "act":
            self.P.op("act", lambda e, a=(out, in_): e.copy(out=a[0], in_=a[1]), r, w)
        else:
            self.P.op(eng, lambda e, a=(out, in_): e.tensor_copy(out=a[0], in_=a[1]), r, w)

    def red(self, out, in_, op, r=(), w=()):
        self.P.op("dve", lambda e, a=(out, in_, op): e.tensor_reduce(out=a[0], in_=a[1], axis=AX.X, op=a[2]), r, w)

    def recip(self, out, in_, r=(), w=()):
        self.P.op("dve", lambda e, a=(out, in_): e.reciprocal(out=a[0], in_=a[1]), r, w)

    def memset(self, eng, ap, val, w=()):
        self.P.op(eng, lambda e, a=(ap, val): e.memset(a[0], a[1]), (), w)

    def dma(self, eng, out, in_, r=(), w=(), **kw):
        self.P.op(eng, lambda e, a=(out, in_, kw): e.dma_start(out=a[0], in_=a[1], **a[2]), r, w, dma=True)

    def load_w(self, dst, src, KC, N, key, rows=128):
        for k in range(KC):
            for c0 in range(0, N, 2048):
                c1 = min(N, c0 + 2048)
                self.dma("pool", dst[:rows, k, c0:c1], src[k * rows:(k + 1) * rows, c0:c1], w=[key + "%d" % k] if c0 == 0 else [key + "%d_%d" % (k, c0)])

    def load_w_keys(self, key, KC, N):
        ks = []
        for k in range(KC):
            for c0 in range(0, N, 2048):
                ks.append(key + "%d" % k if c0 == 0 else key + "%d_%d" % (k, c0))
        return ks


class Tile:
    def __init__(self, kind, n, pos, first, last, b=0, i=0, s=0):
        self.kind, self.n, self.pos, self.first, self.last, self.b, self.i, self.s = kind, n, pos, first, last, b, i, s


def make_tiles():
    tiles = []
    for b in range(NB):
        for i in range(17):
            n = 16 if i == 0 else 128
            pos = 0 if i == 0 else 16 + 128 * (i - 1)
            tiles.append(Tile("p", n, pos, i == 0, i == 16, b=b, i=i))
    for s in range(NS):
        tiles.append(Tile("s", 1, 8192, True, True, s=s))
    return tiles


def build_program(phases=(1, 2, 3, 4), debug_tiles=None):
    nc = bass.Bass("TRN2", target_bir_lowering=False)

    def din(name, shape, dtype=F32):
        return nc.dram_tensor(name, list(shape), dtype, kind="ExternalInput").ap()

    def dout(name, shape):
        return nc.dram_tensor(name, list(shape), F32, kind="ExternalOutput").ap()

    def dscr(name, shape):
        return nc.dram_tensor(name, list(shape), F32, kind="ExternalOutput" if DEBUG_SCRATCH else "Internal").ap()

    I = {}
    for name, shape in [
        ("x_prompt", (NB, 2048, D)), ("x_sample", (NS, D)), ("state_delta_S", (NS, 8, 128, 128)),
        ("state_delta_conv", (NS, 3, NQKV)), ("state_ffn_conv", (2, NS, 2, DFF2)),
        ("cache_kv_latent", (10240 * 128, 256)), ("cache_k_rope", (10240 * 128, 64)),
        ("meta_tokens", (16, D)), ("a_norm_pre", (1, D)), ("a_norm_post", (1, D)), ("a_w_in", (D, GIN)),
        ("a_conv_w", (4, NQKV)), ("a_log", (1, 8)), ("a_dt_bias", (1, 8)), ("a_out_norm", (1, 128)),
        ("a_w_out", (D, D)), ("kv_norm", (1, D)), ("kv_w_a", (D, 320)), ("kv_a_norm", (1, 256)),
        ("kv_w_uk", (256, 1024)), ("kv_w_uv", (256, 1024)), ("b_norm_pre", (1, D)), ("b_norm_post", (1, D)),
        ("b_w_q_a", (D, 384)), ("b_q_a_norm", (1, 384)), ("b_w_q_b", (384, 1536)), ("b_w_out", (D, D)),
        ("f_norm_pre", (2, D)), ("f_norm_post", (2, D)), ("f_w_up", (2, D, DFF2)), ("f_conv_w", (2, 3, DFF2)),
        ("f_conv_b", (2, DFF2)), ("f_w_down", (2, DFF, D)),
        ("c_ident", (128, 128)), ("c_triu", (128, 128)), ("c_lstrict", (128, 128)), ("c_lincl", (128, 128)),
        ("c_negmask", (128, 128)), ("c_rope", (2065, 64)), ("c_iota", (128, 2)),
    ]:
        I[name] = din(name, shape)
    I["page_table"] = din("page_table", (8, NS * 8), I32)
    O = {}
    for name, shape in [
        ("y_prompt", (NB, 2048, D)), ("y_sample", (NS, D)), ("p_delta_S", (NB, 8, 128, 128)),
        ("p_delta_conv", (NB, 3, NQKV)), ("p_ffn_conv", (2, NB, 2, DFF2)), ("p_kv_latent", (NB, LP, 256)),
        ("p_k_rope", (NB, LP, 64)), ("s_delta_S", (NS, 8, 128, 128)), ("s_delta_conv", (NS, 3, NQKV)),
        ("s_ffn_conv", (2, NS, 2, DFF2)), ("s_kv_latent", (NS, 256)), ("s_k_rope", (NS, 64)),
    ]:
        O[name] = dout(name, shape)
    xsp = {k: dscr("xsp%d" % k, (NB, LP, D)) for k in (1, 2, 3)}
    xss = {k: dscr("xss%d" % k, (NS, D)) for k in (1, 2, 3)}
    tiles = make_tiles() if debug_tiles is None else debug_tiles(make_tiles())
    ges = ExitStack()
    GS = SemState(nc, ges)

    def x_in(ph, t):
        if t.kind == "p":
            if ph == 1:
                return I["meta_tokens"] if t.i == 0 else I["x_prompt"][t.b, 128 * (t.i - 1):128 * t.i, :]
            return xsp[HACK_SRC.get(ph, ph - 1)][t.b, t.pos:t.pos + t.n, :]
        return I["x_sample"][t.s:t.s + 1, :] if ph == 1 else xss[ph - 1][t.s:t.s + 1, :]

    def x_out(ph, t):
        if t.kind == "p":
            if ph == 4:
                return None if t.i == 0 else O["y_prompt"][t.b, 128 * (t.i - 1):128 * t.i, :]
            return xsp[ph][t.b, t.pos:t.pos + t.n, :]
        return O["y_sample"][t.s:t.s + 1, :] if ph == 4 else xss[ph][t.s:t.s + 1, :]

    class Phase:
        def __init__(self, name):
            self.name = name
            self.es = ExitStack()
            self.P = Prog(nc, GS)
            self.c = Ctx(nc, self.P)
            self.cnt = 0
            c = self.c
            self.ps = self.es.enter_context(nc.psum_tensor("ps_" + name, [128, 4096], F32))
            self.psb = self.ps[:].bitcast(BF16)
            self.ident_f = self.sb("ident_f", [128, 128], F32)
            self.ident_b = self.sb("ident_b", [128, 128], BF16)
            c.dma("sp", self.ident_f[:], I["c_ident"], w=["ident_f"])
            c.dma("pool", self.ident_b[:], I["c_ident"], w=["ident_b"])
            self.x = [self.sb("x%d" % j, [128, D], F32) for j in range(2)]
            self.junk = self.sb("junk", [128, D], F32)
            self.hn = self.sb("hn", [128, D], BF16)
            self.hT = self.sb("hT", [128, 8, 128], BF16)
            self.st = self.sb("st", [128, 16], F32)
            self.tmp = self.sb("tmpx", [128, D], F32)

        def sb(self, name, shape, dtype):
            return self.es.enter_context(nc.sbuf_tensor(self.name + "_" + name, list(shape), dtype))

        def bank(self, k, nb=1):
            return self.ps[:, 512 * k:512 * (k + nb)]

        def bankb(self, k):
            return self.psb[:, 1024 * k:1024 * (k + 1)]

        def gain(self, name, src_row):
            g = self.sb(name, [128, src_row.shape[-1]], F32)
            self.c.dma("sp", g[:], src_row.partition_broadcast(128), w=[name])
            return g

        def load_x(self, ph, t, slot):
            self.c.dma("sp", self.x[slot][:t.n, :], x_in(ph, t), w=["x%d" % slot])

        def rstd(self, src, n, width, col, r):
            c = self.c
            st = self.st
            c.act(self.junk[:n, :width], src, AF.Square, r=r, w=["junk", "st%d" % col], accum_out=st[:n, col:col + 1])
            c.act(st[:n, col:col + 1], st[:n, col:col + 1], AF.Sqrt, r=["st%d" % col], w=["st%d" % col], scale=1.0 / width, bias=EPS)
            c.recip(st[:n, col:col + 1], st[:n, col:col + 1], r=["st%d" % col], w=["st%d" % col])

        def norm_T(self, xap, xkey, gain, gkey, n, bk, width=D, hn=None, hT=None, hkey="hT", col=0):
            c = self.c
            hn = self.hn if hn is None else hn
            hT = self.hT if hT is None else hT
            KC = width // 128
            self.rstd(xap, n, width, col, [xkey])
            c.stt(hn[:n, :width], xap, self.st[:n, col:col + 1], gain[:n, :width], ALU.mult, ALU.mult, r=[xkey, "st%d" % col, gkey], w=["hn" + hkey])
            pb = self.bankb(bk)
            for k in range(KC):
                c.tr(pb[:, k * 128:k * 128 + n], hn[:n, k * 128:(k + 1) * 128], self.ident_b[:n, :n], r=["hn" + hkey, "ident_b"], w=["bank%d" % bk])
            c.cp("act", hT[:, 0:KC, :n], pb[:, 0:KC * 128].rearrange("p (k t) -> p k t", t=128)[:, :, :n], r=["bank%d" % bk], w=[hkey])

        def post_norm_res(self, obk, n, gain, gkey, xslot, col=1):
            c = self.c
            o = self.bank(obk, 2)[:n, :]
            okeys = ["bank%d" % obk, "bank%d" % (obk + 1)]
            self.rstd(o, n, D, col, okeys)
            c.stt(self.tmp[:n, :], o, self.st[:n, col:col + 1], gain[:n, :], ALU.mult, ALU.mult, r=okeys + ["st%d" % col, gkey], w=["tmpx"])
            c.tt("dve", self.x[xslot][:n, :], self.x[xslot][:n, :], self.tmp[:n, :], ALU.add, r=["tmpx", "x%d" % xslot], w=["x%d" % xslot])

        def store_x(self, ph, t, slot):
            dst = x_out(ph, t)
            if dst is not None:
                self.c.dma("sp", dst, self.x[slot][:t.n, :], r=["x%d" % slot], w=["xout"])

        def finish(self):
            self.P.emit()
            self.es.close()

    def phase_gdn():
        ph = Phase("g")
        c = ph.c
        w_in = ph.sb("w_in", [128, 8, GIN], BF16)
        w_out = ph.sb("w_out", [128, 8, D], BF16)
        c.load_w(w_in, I["a_w_in"], 8, GIN, "w_in")
        c.load_w(w_out, I["a_w_out"], 8, D, "w_out")
        K_WIN = c.load_w_keys("w_in", 8, GIN)
        K_WOUT = c.load_w_keys("w_out", 8, D)
        g_pre = ph.gain("g_pre", I["a_norm_pre"])
        g_post = ph.gain("g_post", I["a_norm_post"])
        g_on = ph.gain("g_on", I["a_out_norm"])
        alog = ph.gain("alog", I["a_log"])
        dtb = ph.gain("dtb", I["a_dt_bias"])
        negA = ph.sb("negA", [128, 8], F32)
        c.act(negA[:], alog[:], AF.Exp, r=["alog"], w=["negA"])
        c.ts("dve", negA[:], negA[:], -1.0, None, ALU.mult, r=["negA"], w=["negA"])
        triu = ph.sb("triu", [128, 128], F32)
        lstrict = ph.sb("lstrict", [128, 128], F32)
        lincl = ph.sb("lincl", [128, 128], F32)
        ones = ph.sb("ones", [128, 128], F32)
        nones = ph.sb("nones", [128, 128], F32)
        c.dma("sp", triu[:], I["c_triu"], w=["triu"])
        c.dma("sp", lstrict[:], I["c_lstrict"], w=["lstrict"])
        c.dma("sp", lincl[:], I["c_lincl"], w=["lincl"])
        c.memset("dve", ones[:], 1.0, w=["ones"])
        c.memset("dve", nones[:], -1.0, w=["nones"])
        cwrow = ph.sb("cwrow", [96, 128], F32)
        cw = ph.sb("cw", [128, 24, 4], F32)
        c.dma("sp", cwrow[:], I["a_conv_w"].rearrange("j (c p) -> (j c) p", p=128), w=["cwrow"])
        c.tr(ph.bank(0)[:, 0:96], cwrow[:96, :], ph.ident_f[:96, :96], r=["cwrow", "ident_f"], w=["bank0"])
        c.cp("dve", cw[:], ph.bank(0)[:, 0:96].rearrange("p (j c) -> p c j", j=4), r=["bank0"], w=["cw"])

        xbuf = ph.sb("xbuf", [128, 24, 131], F32)
        ybuf = [ph.sb("ybuf%d" % j, [128, 128], F32) for j in range(8)]
        qkvT = ph.sb("qkvT", [128, 24, 128], BF16)
        tok = ph.sb("tok", [128, 24, 128], BF16)
        zs = ph.tmp
        gb = ph.sb("gb", [128, 64], F32)
        srow = ph.sb("srow", [72, 128], F32)
        crow = ph.sb("crow", [4, 1024], F32)
        kn = ph.sb("kn", [128, 8, 128], BF16)
        kbg = ph.sb("kbg", [128, 8, 128], BF16)
        ktl = ph.sb("ktl", [128, 8, 128], BF16)
        qn = ph.sb("qn", [128, 8, 128], BF16)
        qd = ph.sb("qd", [128, 8, 128], BF16)
        vb = ph.sb("vb", [128, 8, 128], BF16)
        knT = ph.sb("knT", [128, 8, 128], BF16)
        qnT = ph.sb("qnT", [128, 8, 128], BF16)
        qdT = ph.sb("qdT", [128, 8, 128], BF16)
        Gh = [ph.sb("Gh%d" % j, [128, 128], F32) for j in range(2)]
        Em = ph.sb("Em", [128, 4, 128], F32)
        Ee = ph.sb("Ee", [128, 4, 128], F32)
        t4 = ph.sb("t4", [128, 4, 128], F32)
        Am = ph.sb("Am", [128, 4, 128], F32)
        At = ph.sb("At", [128, 4, 128], F32)
        attn = ph.sb("attn", [128, 4, 128], BF16)
        attnT = ph.sb("attnT", [128, 4, 128], BF16)
        Pb = [ph.sb("Pb%d" % j, [128, 4, 128], F32) for j in range(2)]
        Qb = [ph.sb("Qb%d" % j, [128, 4, 128], F32) for j in range(2)]
        Xb = [ph.sb("Xb%d" % j, [128, 4, 128], F32) for j in range(2)]
        Xf = ph.sb("Xf", [128, 4, 128], BF16)
        nwT = ph.sb("nwT", [128, 4, 128], BF16)
        vnew = ph.sb("vnew", [128, 4, 128], BF16)
        S = ph.sb("S", [128, 8, 128], F32)
        Sb = ph.sb("Sb", [128, 8, 128], BF16)
        otok = ph.sb("otok", [128, 8, 128], F32)
        og = ph.sb("og", [128, D], BF16)
        oT = ph.sb("oT", [128, 8, 128], BF16)

        BETA, G, GC, GL, EG, EGL, EGLAST, RINV, FKBG, FKT, FQN, FQD, OSS = 0, 8, 16, 24, 32, 40, 48, 56, 72, 80, 88, 96, 104
        gb2 = ph.sb("gb2", [128, 112], F32)

        for ti, t in enumerate(tiles):
            n = t.n
            xs = ti % 2
            ph.load_x(1, t, xs)
            xk = "x%d" % xs
            x = ph.x[xs]
            if t.first:
                if t.kind == "p":
                    c.memset("pool", xbuf[:, :, 0:3], 0.0, w=["xbuf%d" % g for g in range(6)])
                    c.memset("pool", S[:], 0.0, w=["S"])
                    c.memset("pool", Sb[:], 0.0, w=["Sb"])
                else:
                    c.dma("sp", srow[:], I["state_delta_conv"][t.s].rearrange("j (c p) -> (j c) p", p=128), w=["srow"])
                    c.tr(ph.bank(0)[:, 0:72], srow[:72, :], ph.ident_f[:72, :72], r=["srow", "ident_f"], w=["bank0"])
                    c.cp("dve", xbuf[:, :, 0:3], ph.bank(0)[:, 0:72].rearrange("p (j c) -> p c j", j=3), r=["bank0"], w=["xbuf%d" % g for g in range(6)])
                    c.dma("sp", S[:], I["state_delta_S"][t.s].rearrange("h k v -> k h v"), w=["S"])
                    c.cp("pool", Sb[:], S[:], r=["S"], w=["Sb"])
                    c.dma("sp", O["s_delta_conv"][t.s, 0:2, :], I["state_delta_conv"][t.s, 1:3, :], w=["sdc_out"])
            ph.norm_T(x[:n, :], xk, g_pre, "g_pre", n, 1)
            for cg in range(6):
                bk = 2 + cg % 2
                q_ = cg % 2
                for j in range(4):
                    ch = cg * 4 + j
                    for k in range(8):
                        c.mm(ph.bank(bk)[:, j * 128:j * 128 + n], w_in[:, k, ch * 128:(ch + 1) * 128], ph.hT[:, k, :n], start=(k == 0), stop=(k == 7), r=["hT"] + K_WIN, w=["bank%d" % bk])
                c.cp("act", xbuf[:, cg * 4:cg * 4 + 4, 3:3 + n], ph.bank(bk).rearrange("p (j t) -> p j t", t=128)[:, :, :n], r=["bank%d" % bk], w=["xbuf%d" % cg])
                for j in range(4):
                    ch = cg * 4 + j
                    c.act(ybuf[q_ * 4 + j][:, :n], ph.bank(bk)[:, j * 128:j * 128 + n], AF.Identity, r=["bank%d" % bk, "cw"], w=["ybuf%d" % (q_ * 4 + j)], scale=cw[:, ch, 3:4])
                for tap in range(3):
                    for j in range(4):
                        ch = cg * 4 + j
                        yk = "ybuf%d" % (q_ * 4 + j)
                        c.stt(ybuf[q_ * 4 + j][:, :n], xbuf[:, ch, tap:tap + n], cw[:, ch, tap:tap + 1], ybuf[q_ * 4 + j][:, :n], ALU.mult, ALU.add, r=["xbuf%d" % cg, "cw", yk], w=[yk])
                for j in range(4):
                    ch = cg * 4 + j
                    c.act(qkvT[:, ch, :n], ybuf[q_ * 4 + j][:, :n], AF.Silu, r=["ybuf%d" % (q_ * 4 + j)], w=["qkvT%d" % (ch // 8)])
                if not t.last:
                    c.cp("act", xbuf[:, cg * 4:cg * 4 + 4, 0:3], xbuf[:, cg * 4:cg * 4 + 4, n:n + 3], r=["xbuf%d" % cg], w=["xbuf%d" % cg])
            for hf in range(2):
                for k in range(8):
                    c.mm(ph.bank(hf)[:n, :], ph.hT[:, k, :n], w_in[:, k, 3072 + hf * 512:3072 + (hf + 1) * 512], start=(k == 0), stop=(k == 7), r=["hT"] + K_WIN, w=["bank%d" % hf])
            c.act(zs[:n, :], ph.bank(0, 2)[:n, :], AF.Silu, r=["bank0", "bank1"], w=["tmpx"])
            for k in range(8):
                c.mm(ph.bank(4)[:n, 0:16], ph.hT[:, k, :n], w_in[:, k, 4096:4112], start=(k == 0), stop=(k == 7), r=["hT"] + K_WIN, w=["bank4"])
            c.act(gb2[:n, BETA:BETA + 8], ph.bank(4)[:n, 8:16], AF.Sigmoid, r=["bank4"], w=["beta"])
            c.tt("dve", gb2[:n, G:G + 8], ph.bank(4)[:n, 0:8], dtb[:n, :], ALU.add, r=["bank4", "dtb"], w=["g"])
            c.act(gb2[:n, G:G + 8], gb2[:n, G:G + 8], AF.Exp, r=["g"], w=["g"])
            c.act(gb2[:n, G:G + 8], gb2[:n, G:G + 8], AF.Ln, r=["g"], w=["g"], bias=1.0)
            c.tt("dve", gb2[:n, G:G + 8], gb2[:n, G:G + 8], negA[:n, :], ALU.mult, r=["g", "negA"], w=["g"])
            if t.last:
                M = min(n, 3)
                for rd in range(3):
                    for hf in range(2):
                        for k in range(8):
                            c.mm(ph.bank(hf)[:M, :], ph.hT[:, k, n - M:n], w_in[:, k, rd * 1024 + hf * 512:rd * 1024 + (hf + 1) * 512], start=(k == 0), stop=(k == 7), r=["hT"] + K_WIN, w=["bank%d" % hf])
                    c.cp("act", crow[:M, :], ph.bank(0, 2)[:M, :], r=["bank0", "bank1"], w=["crow"])
                    dst = O["p_delta_conv"][t.b, 3 - M:3, rd * 1024:(rd + 1) * 1024] if t.kind == "p" else O["s_delta_conv"][t.s, 3 - M:3, rd * 1024:(rd + 1) * 1024]
                    c.dma("sp", dst, crow[:M, :], r=["crow"], w=["dc_out"])
            for g3 in range(3):
                pb = ph.bankb(5 + g3)
                for j in range(8):
                    ch = g3 * 8 + j
                    c.tr(pb[:n, j * 128:(j + 1) * 128], qkvT[:, ch, :n], ph.ident_b[:, :], r=["qkvT%d" % g3, "ident_b"], w=["bank%d" % (5 + g3)])
                c.cp("dve" if g3 == 1 else "act", tok[:n, g3 * 8:(g3 + 1) * 8, :], pb[:n, :].rearrange("p (j d) -> p j d", d=128), r=["bank%d" % (5 + g3)], w=["tok%d" % g3])
            for ch in range(16):
                c.act(ph.junk[:n, 0:128], tok[:n, ch, :], AF.Square, r=["tok%d" % (ch // 8)], w=["junk", "rinv"], accum_out=gb2[:n, RINV + ch:RINV + ch + 1])
            c.act(gb2[:n, RINV:RINV + 16], gb2[:n, RINV:RINV + 16], AF.Sqrt, r=["rinv"], w=["rinv"], bias=EPS)
            c.recip(gb2[:n, RINV:RINV + 16], gb2[:n, RINV:RINV + 16], r=["rinv"], w=["rinv"])
            c.mm(ph.bank(4)[:n, 16:24], triu[:n, :n], gb2[:n, G:G + 8], r=["triu", "g"], w=["bank4"])
            c.mm(ph.bank(4)[:, 24:32], ones[:n, :], gb2[:n, G:G + 8], r=["ones", "g"], w=["bank4"])
            c.cp("dve", gb2[:n, GC:GC + 8], ph.bank(4)[:n, 16:24], r=["bank4"], w=["gc"])
            c.cp("dve", gb2[:, GL:GL + 8], ph.bank(4)[:, 24:32], r=["bank4"], w=["gl"])
            c.act(gb2[:n, EG:EG + 8], gb2[:n, GC:GC + 8], AF.Exp, r=["gc"], w=["eg"])
            c.tt("dve", gb2[:n, EGL:EGL + 8], gb2[:n, GL:GL + 8], gb2[:n, GC:GC + 8], ALU.subtract, r=["gl", "gc"], w=["egl"])
            c.act(gb2[:n, EGL:EGL + 8], gb2[:n, EGL:EGL + 8], AF.Exp, r=["egl"], w=["egl"])
            c.act(gb2[:, EGLAST:EGLAST + 8], gb2[:, GL:GL + 8], AF.Exp, r=["gl"], w=["eglast"])
            c.tt("dve", gb2[:n, FKBG:FKBG + 8], gb2[:n, RINV + 8:RINV + 16], gb2[:n, BETA:BETA + 8], ALU.mult, r=["rinv", "beta"], w=["fkbg"])
            c.tt("dve", gb2[:n, FKBG:FKBG + 8], gb2[:n, FKBG:FKBG + 8], gb2[:n, EG:EG + 8], ALU.mult, r=["fkbg", "eg"], w=["fkbg"])
            c.tt("dve", gb2[:n, FKT:FKT + 8], gb2[:n, RINV + 8:RINV + 16], gb2[:n, EGL:EGL + 8], ALU.mult, r=["rinv", "egl"], w=["fkt"])
            c.ts("dve", gb2[:n, FQN:FQN + 8], gb2[:n, RINV:RINV + 8], 128.0 ** -0.5, None, ALU.mult, r=["rinv"], w=["fqn"])
            c.tt("dve", gb2[:n, FQD:FQD + 8], gb2[:n, FQN:FQN + 8], gb2[:n, EG:EG + 8], ALU.mult, r=["fqn", "eg"], w=["fqd"])

            def bc(col):
                return gb2[:n, col:col + 8].unsqueeze(2).to_broadcast([n, 8, 128])
            c.tt("dve", kn[:n], tok[:n, 8:16, :], bc(RINV + 8), ALU.mult, r=["tok1", "rinv"], w=["kn"])
            c.tt("dve", kbg[:n], tok[:n, 8:16, :], bc(FKBG), ALU.mult, r=["tok1", "fkbg"], w=["kbg"])
            c.tt("pool", ktl[:n], tok[:n, 8:16, :], bc(FKT), ALU.mult, r=["tok1", "fkt"], w=["ktl"])
            c.tt("dve", qn[:n], tok[:n, 0:8, :], bc(FQN), ALU.mult, r=["tok0", "fqn"], w=["qn"])
            c.tt("pool", qd[:n], tok[:n, 0:8, :], bc(FQD), ALU.mult, r=["tok0", "fqd"], w=["qd"])
            c.tt("pool", vb[:n], tok[:n, 16:24, :], bc(BETA), ALU.mult, r=["tok2", "beta"], w=["vb"])
            for bi, (src, dst, sk, dk_) in enumerate([(kn, knT, "kn", "knT"), (qn, qnT, "qn", "qnT"), (qd, qdT, "qd", "qdT")]):
                bk = 5 + bi
                pb = ph.bankb(bk)
                for h in range(8):
                    c.tr(pb[:, h * 128:h * 128 + n], src[:n, h, :], ph.ident_b[:n, :n], r=[sk, "ident_b"], w=["bank%d" % bk])
                c.cp("act" if bi != 1 else "dve", dst[:, :, :n], pb[:, :].rearrange("p (h t) -> p h t", t=128)[:, :, :n], r=["bank%d" % bk], w=[dk_])
            nlev = 0
            while (1 << (nlev + 1)) < n:
                nlev += 1
            for hg in range(2):
                hs = [hg * 4 + j for j in range(4)]
                for j, h in enumerate(hs):
                    G_ = Gh[j % 2]
                    gk = "Gh%d" % (j % 2)
                    c.ts("dve", G_[:n, :n], triu[:n, :n], gb2[:n, G + h:G + h + 1], None, ALU.mult, r=["triu", "g"], w=[gk])
                    c.mm(ph.bank(2)[:n, j * 128:j * 128 + n], G_[:n, :n], ones[:n, :n], start=True, stop=False, r=[gk, "ones"], w=["bank2"])
                    c.mm(ph.bank(2)[:n, j * 128:j * 128 + n], nones[:n, :n], G_[:n, :n], start=False, stop=True, r=[gk, "nones"], w=["bank2"])
                    c.mm(ph.bank(3)[:n, j * 128:j * 128 + n], knT[:, h, :n], knT[:, h, :n], r=["knT"], w=["bank3"])
                    c.mm(ph.bank(4)[:n, j * 128:j * 128 + n], qnT[:, h, :n], knT[:, h, :n], r=["qnT", "knT"], w=["bank4"])

                def v4(bk):
                    return ph.bank(bk)[:n, :].rearrange("p (j t) -> p j t", t=128)[:, :, :n]
                c.ts("dve", Em[:n, :, :n], v4(2), 0.0, None, ALU.min, r=["bank2"], w=["Em"])
                c.act(Ee[:n, :, :n], Em[:n, :, :n], AF.Exp, r=["Em"], w=["Ee"])
                c.tt("dve", t4[:n, :, :n], v4(3), Ee[:n, :, :n], ALU.mult, r=["bank3", "Ee"], w=["t4"])
                for j, h in enumerate(hs):
                    c.stt(Am[:n, j, :n], t4[:n, j, :n], gb2[:n, BETA + h:BETA + h + 1], lstrict[:n, :n], ALU.mult, ALU.mult, r=["t4", "beta", "lstrict"], w=["Am"])
                c.tt("dve", t4[:n, :, :n], v4(4), Ee[:n, :, :n], ALU.mult, r=["bank4", "Ee", "Am"], w=["t4"])
                c.tt("pool", attn[:n, :, :n], t4[:n, :, :n], lincl[:n, :n].unsqueeze(1).to_broadcast([n, 4, n]), ALU.mult, r=["t4", "lincl"], w=["attn"])
                pb = ph.bankb(6)
                for j in range(4):
                    c.tr(ph.bank(5)[:n, j * 128:j * 128 + n], Am[:n, j, :n], ph.ident_f[:n, :n], r=["Am", "ident_f"], w=["bank5"])
                    c.tr(pb[:n, j * 128:j * 128 + n], attn[:n, j, :n], ph.ident_b[:n, :n], r=["attn", "ident_b"], w=["bank6"])
                c.cp("act", At[:n, :, :n], v4(5), r=["bank5"], w=["At"])
                c.cp("act", attnT[:n, :, :n], pb[:n, 0:512].rearrange("p (j t) -> p j t", t=128)[:, :, :n], r=["bank6"], w=["attnT"])
                c.tt("dve", Xb[0][:n, :, :n], ph.ident_f[:n, :n].unsqueeze(1).to_broadcast([n, 4, n]), At[:n, :, :n], ALU.subtract, r=["ident_f", "At"], w=["Xb0"])
                Pc, Qc, Pk, Qk = At, Am, "At", "Am"
                xi = 0
                for lv in range(1, nlev + 1):
                    lastl = lv == nlev
                    Qn_, Qnk = Qb[lv % 2], "Qb%d" % (lv % 2)
                    Pn_, Pnk = Pb[lv % 2], "Pb%d" % (lv % 2)
                    for j in range(4):
                        c.mm(ph.bank(6)[:n, j * 128:j * 128 + n], Pc[:n, j, :n], Qc[:n, j, :n], r=[Pk, Qk], w=["bank6"])
                        if not lastl:
                            c.mm(ph.bank(7)[:n, j * 128:j * 128 + n], Qc[:n, j, :n], Pc[:n, j, :n], r=[Pk, Qk], w=["bank7"])
                    c.cp("act", Qn_[:n, :, :n], v4(6), r=["bank6"], w=[Qnk])
                    if not lastl:
                        c.cp("dve", Pn_[:n, :, :n], v4(7), r=["bank7"], w=[Pnk])
                    for j in range(4):
                        c.mm(ph.bank(2)[:n, j * 128:j * 128 + n], Qn_[:n, j, :n], Xb[xi][:n, j, :n], r=[Qnk, "Xb%d" % xi], w=["bank2"])
                    c.tt("dve", Xb[1 - xi][:n, :, :n], v4(2), Xb[xi][:n, :, :n], ALU.add, r=["bank2", "Xb%d" % xi], w=["Xb%d" % (1 - xi)])
                    xi = 1 - xi
                    Pc, Qc, Pk, Qk = Pn_, Qn_, Pnk, Qnk
                c.cp("act", Xf[:n, :, :n], Xb[xi][:n, :, :n], r=["Xb%d" % xi], w=["Xf"])
                X_ = Xf
                Xk = "Xf"
                for j, h in enumerate(hs):
                    c.mm(ph.bank(3)[:, j * 128:j * 128 + n], kbg[:n, h, :], X_[:n, j, :n], r=["kbg", Xk], w=["bank3"])
                c.act(nwT[:, :, :n], ph.bank(3).rearrange("p (j t) -> p j t", t=128)[:, :, :n], AF.Copy, r=["bank3"], w=["nwT"], scale=-1.0)
                for j, h in enumerate(hs):
                    c.mm(ph.bank(4)[:n, j * 128:(j + 1) * 128], X_[:n, j, :n], vb[:n, h, :], start=True, stop=False, r=[Xk, "vb"], w=["bank4"])
                    c.mm(ph.bank(4)[:n, j * 128:(j + 1) * 128], nwT[:, j, :n], Sb[:, h, :], start=False, stop=True, r=["nwT", "Sb"], w=["bank4"])
                c.cp("act", vnew[:n], ph.bank(4)[:n, :].rearrange("p (j d) -> p j d", d=128), r=["bank4"], w=["vnew"])
                for j, h in enumerate(hs):
                    c.mm(ph.bank(5)[:n, j * 128:(j + 1) * 128], qdT[:, h, :n], Sb[:, h, :], start=True, stop=False, r=["qdT", "Sb"], w=["bank5"])
                    c.mm(ph.bank(5)[:n, j * 128:(j + 1) * 128], attnT[:n, j, :n], vnew[:n, j, :], start=False, stop=True, r=["attnT", "vnew"], w=["bank5"])
                c.cp("dve", otok[:n, hg * 4:hg * 4 + 4, :], ph.bank(5)[:n, :].rearrange("p (j d) -> p j d", d=128), r=["bank5"], w=["otok"])
                for j, h in enumerate(hs):
                    c.mm(ph.bank(6)[:, j * 128:(j + 1) * 128], ktl[:n, h, :], vnew[:n, j, :], r=["ktl", "vnew"], w=["bank6"])
                for j, h in enumerate(hs):
                    c.stt(S[:, h, :], S[:, h, :], gb2[:, EGLAST + h:EGLAST + h + 1], ph.bank(6)[:, j * 128:(j + 1) * 128], ALU.mult, ALU.add, r=["S", "eglast", "bank6", "Sb"], w=["S"])
                c.cp("pool", Sb[:, hg * 4:hg * 4 + 4, :], S[:, hg * 4:hg * 4 + 4, :], r=["S"], w=["Sb"])
            for h in range(8):
                c.act(ph.junk[:n, 0:128], otok[:n, h, :], AF.Square, r=["otok"], w=["junk", "oss"], accum_out=gb2[:n, OSS + h:OSS + h + 1])
            c.act(gb2[:n, OSS:OSS + 8], gb2[:n, OSS:OSS + 8], AF.Sqrt, r=["oss"], w=["oss"], scale=1.0 / 128, bias=EPS)
            c.recip(gb2[:n, OSS:OSS + 8], gb2[:n, OSS:OSS + 8], r=["oss"], w=["oss"])
            c.tt("dve", otok[:n], otok[:n], bc(OSS), ALU.mult, r=["otok", "oss"], w=["otok"])
            c.tt("dve", otok[:n], otok[:n], g_on[:n, :].unsqueeze(1).to_broadcast([n, 8, 128]), ALU.mult, r=["otok", "g_on"], w=["otok"])
            c.tt("dve", og[:n, :], otok[:n].rearrange("p h d -> p (h d)"), zs[:n, :], ALU.mult, r=["otok", "tmpx"], w=["og"])
            pb = ph.bankb(7)
            for k in range(8):
                c.tr(pb[:, k * 128:k * 128 + n], og[:n, k * 128:(k + 1) * 128], ph.ident_b[:n, :n], r=["og", "ident_b"], w=["bank7"])
            c.cp("act", oT[:, :, :n], pb[:, :].rearrange("p (k t) -> p k t", t=128)[:, :, :n], r=["bank7"], w=["oT"])
            for hf in range(2):
                for k in range(8):
                    c.mm(ph.bank(hf)[:n, :], oT[:, k, :n], w_out[:, k, hf * 512:(hf + 1) * 512], start=(k == 0), stop=(k == 7), r=["oT"] + K_WOUT, w=["bank%d" % hf])
            ph.post_norm_res(0, n, g_post, "g_post", xs)
            ph.store_x(1, t, xs)
            if t.last:
                dst = O["p_delta_S"][t.b] if t.kind == "p" else O["s_delta_S"][t.s]
                c.dma("sp", dst.rearrange("h k v -> k h v"), S[:], r=["S"], w=["S_out"])
        ph.finish()

    def phase_ffn(L):
        phn = 2 if L == 0 else 4
        ph = Phase("f%d" % L)
        c = ph.c
        w_up = ph.sb("w_up", [128, 8, DFF2], BF16)
        w_dn = ph.sb("w_dn", [128, 22, D], BF16)
        c.load_w(w_up, I["f_w_up"][L], 8, DFF2, "w_up")
        c.load_w(w_dn, I["f_w_down"][L], 22, D, "w_dn")
        K_UP = c.load_w_keys("w_up", 8, DFF2)
        K_DN = c.load_w_keys("w_dn", 22, D)
        g_pre = ph.gain("g_pre", I["f_norm_pre"][L:L + 1, :])
        g_post = ph.gain("g_post", I["f_norm_post"][L:L + 1, :])
        cwrow = [ph.sb("cwrow%d" % j, [88, 128], F32) for j in range(2)]
        cw = ph.sb("cw", [128, 44, 4], F32)
        c.dma("sp", cwrow[0][:], I["f_conv_w"][L, 0:2, :].rearrange("j (c p) -> (j c) p", p=128), w=["cwrow0"])
        c.dma("sp", cwrow[1][0:44, :], I["f_conv_w"][L, 2:3, :].rearrange("j (c p) -> (j c) p", p=128), w=["cwrow1"])
        c.dma("sp", cwrow[1][44:88, :], I["f_conv_b"][L:L + 1, :].rearrange("j (c p) -> (j c) p", p=128), w=["cwrow1b"])
        for j in range(2):
            c.tr(ph.bank(0)[:, j * 88:(j + 1) * 88], cwrow[j][:88, :], ph.ident_f[:88, :88], r=["cwrow%d" % j, "cwrow1b", "ident_f"], w=["bank0"])
        c.cp("dve", cw[:], ph.bank(0)[:, 0:176].rearrange("p (j c) -> p c j", j=4), r=["bank0"], w=["cw"])
        ubuf = ph.sb("ubuf", [128, 44, 130], F32)
        yg = [[ph.sb("yg%d_%d" % (q, j), [128, 128], F32) for j in range(4)] for q in range(2)]
        yv = [[ph.sb("yv%d_%d" % (q, j), [128, 128], F32) for j in range(4)] for q in range(2)]
        mT = ph.sb("mT", [128, 22, 128], BF16)
        crow = ph.sb("crow", [16, 1024], F32)
        ubflat = ubuf[:, :, :].rearrange("p c t -> p (c t)")
        hs = ubflat[:, 0:1408].rearrange("p (b q) -> p b q", q=128)
        hist = ubflat[:, 1408:2816]
        histv = hist.rearrange("p (s j c) -> p c j s", s=16, j=2, c=44)
        UBK = ["ubg%d" % g for g in range(6)] + ["ubv%d" % g for g in range(6)]
        ftiles = [t for t in tiles if t.kind == "p"]
        if any(t.kind == "s" for t in tiles):
            ftiles.append(Tile("S", NS, 8192, True, True))
        GROUPS = [(k * 4, min(4, 22 - k * 4)) for k in range(6)]

        def fx_in(t):
            return xss[phn - 1][0:NS, :] if t.kind == "S" else x_in(phn, t)

        def fx_out(t):
            if t.kind == "S":
                return O["y_sample"][0:NS, :] if phn == 4 else xss[phn][0:NS, :]
            return x_out(phn, t)

        def st_norm(ti):
            t = ftiles[ti]
            xs = ti % 2
            c.dma("sp", ph.x[xs][:t.n, :], fx_in(t), w=["x%d" % xs])
            if t.first:
                if t.kind == "p":
                    c.memset("pool", ubuf[:, :, 0:2], 0.0, w=["ubg%d" % g for g in range(6)] + ["ubv%d" % g for g in range(6)])
                else:
                    c.dma("sp", hs, I["state_ffn_conv"][L].rearrange("s j (c p) -> (s j c) p", p=128).rearrange("(b r) p -> r b p", r=128), w=["hs"] + UBK)
                    for blk in range(11):
                        c.tr(ph.ps[:, blk * 128:(blk + 1) * 128], hs[:, blk, :], ph.ident_f[:, :], r=["hs", "ident_f"], w=["bank%d" % (blk // 4)])
                    c.cp("dve", hist, ph.ps[:, 0:1408], r=["bank0", "bank1", "bank2"], w=["hist"])
                    c.dma("sp", O["s_ffn_conv"][L, :, 0, :], I["state_ffn_conv"][L, :, 1, :], w=["sfc_out"])
            ph.norm_T(ph.x[xs][:t.n, :], "x%d" % xs, g_pre, "g_pre", t.n, 6)

        def st_up_conv(ti):
            t = ftiles[ti]
            n = t.n
            if t.last:
                M = min(n, 2) if t.kind == "p" else n
                for rd in range(6):
                    nh = 2 if rd < 5 else 1
                    for hf in range(nh):
                        for k in range(8):
                            c.mm(ph.bank(hf)[:M, :], ph.hT[:, k, n - M:n], w_up[:, k, rd * 1024 + hf * 512:rd * 1024 + (hf + 1) * 512], start=(k == 0), stop=(k == 7), r=["hT"] + K_UP, w=["bank%d" % hf])
                    c.cp("act", crow[:M, :nh * 512], ph.bank(0, 2)[:M, :nh * 512], r=["bank0", "bank1"], w=["crow"])
                    if t.kind == "p":
                        dst = O["p_ffn_conv"][L, t.b, 2 - M:2, rd * 1024:rd * 1024 + nh * 512]
                    else:
                        dst = O["s_ffn_conv"][L, :, 1, rd * 1024:rd * 1024 + nh * 512]
                    c.dma("sp", dst, crow[:M, :nh * 512], r=["crow"], w=["fc_out"])
            for gi, (c0, cnt) in enumerate(GROUPS):
                q = gi % 2
                bA, bB = 2 + 2 * q, 3 + 2 * q
                for (bk, base) in ((bA, c0), (bB, 22 + c0)):
                    for j in range(cnt):
                        ch = base + j
                        for k in range(8):
                            c.mm(ph.bank(bk)[:, j * 128:j * 128 + n], w_up[:, k, ch * 128:(ch + 1) * 128], ph.hT[:, k, :n], start=(k == 0), stop=(k == 7), r=["hT"] + K_UP, w=["bank%d" % bk])
                if t.kind == "p":
                    c.cp("act", ubuf[:, c0:c0 + cnt, 2:2 + n], ph.bank(bA).rearrange("p (j t) -> p j t", t=128)[:, 0:cnt, :n], r=["bank%d" % bA], w=["ubg%d" % gi])
                    c.cp("act", ubuf[:, 22 + c0:22 + c0 + cnt, 2:2 + n], ph.bank(bB).rearrange("p (j t) -> p j t", t=128)[:, 0:cnt, :n], r=["bank%d" % bB], w=["ubv%d" % gi])
                ys = [(yg[q][j], "yg%d_%d" % (q, j), c0 + j, bA, "ubg%d" % gi) for j in range(cnt)] + [(yv[q][j], "yv%d_%d" % (q, j), 22 + c0 + j, bB, "ubv%d" % gi) for j in range(cnt)]
                for (yb, yk, ch, bk, uk) in ys:
                    j = (ch - c0) % 22
                    c.act(yb[:, :n], ph.bank(bk)[:, j * 128:j * 128 + n], AF.Identity, r=["bank%d" % bk, "cw"], w=[yk], scale=cw[:, ch, 2:3], bias=cw[:, ch, 3:4])
                for tap in range(2):
                    for (yb, yk, ch, bk, uk) in ys:
                        if t.kind == "p":
                            src, sk = ubuf[:, ch, tap:tap + n], uk
                        else:
                            src, sk = histv[:, ch, tap, :n], "hist"
                        c.stt(yb[:, :n], src, cw[:, ch, tap:tap + 1], yb[:, :n], ALU.mult, ALU.add, r=[sk, "cw", yk], w=[yk])
                for j in range(cnt):
                    c.act(yg[q][j][:, :n], yg[q][j][:, :n], AF.Silu, r=["yg%d_%d" % (q, j)], w=["yg%d_%d" % (q, j)])
                for j in range(cnt):
                    c.tt("dve", mT[:, c0 + j, :n], yg[q][j][:, :n], yv[q][j][:, :n], ALU.mult, r=["yg%d_%d" % (q, j), "yv%d_%d" % (q, j)], w=["mT%d" % gi])
                if t.kind == "p" and not t.last:
                    c.cp("act", ubuf[:, c0:c0 + cnt, 0:2], ubuf[:, c0:c0 + cnt, n:n + 2], r=["ubg%d" % gi], w=["ubg%d" % gi])
                    c.cp("act", ubuf[:, 22 + c0:22 + c0 + cnt, 0:2], ubuf[:, 22 + c0:22 + c0 + cnt, n:n + 2], r=["ubv%d" % gi], w=["ubv%d" % gi])

        def st_down(ti):
            t = ftiles[ti]
            n = t.n
            xs = ti % 2
            for hf in range(2):
                for k in range(22):
                    c.mm(ph.bank(hf)[:n, :], mT[:, k, :n], w_dn[:, k, hf * 512:(hf + 1) * 512], start=(k == 0), stop=(k == 21), r=["mT%d" % (k // 4)] + K_DN, w=["bank%d" % hf])
            ph.post_norm_res(0, n, g_post, "g_post", xs)
            dst = fx_out(t)
            if dst is not None:
                c.dma("sp", dst, ph.x[xs][:n, :], r=["x%d" % xs], w=["xout"])

        st_norm(0)
        for ti in range(len(ftiles)):
            st_up_conv(ti)
            if ti + 1 < len(ftiles):
                st_norm(ti + 1)
            st_down(ti)
        ph.finish()


    def phase_mla(sample):
        ph = Phase("ms" if sample else "mp")
        c = ph.c
        kvwa = ph.sb("kvwa", [128, 8, 320], BF16)
        wqa = ph.sb("wqa", [128, 8, 384], BF16)
        wqb = ph.sb("wqb", [128, 3, 1536], BF16)
        bwo = ph.sb("bwo", [128, 8, D], BF16)
        wuv = ph.sb("wuv", [128, 2, 1024], BF16)
        wukr = ph.sb("wukr", [128, 2, 1024], BF16)
        wukT = ph.sb("wukT", [128, 8, 256], BF16)
        c.load_w(kvwa, I["kv_w_a"], 8, 320, "kvwa")
        c.load_w(wqa, I["b_w_q_a"], 8, 384, "wqa")
        c.load_w(wqb, I["b_w_q_b"], 3, 1536, "wqb")
        c.load_w(bwo, I["b_w_out"], 8, D, "bwo")
        c.load_w(wuv, I["kv_w_uv"], 2, 1024, "wuv")
        c.load_w(wukr, I["kv_w_uk"], 2, 1024, "wukr")
        K_KVWA, K_WQA, K_WQB = c.load_w_keys("kvwa", 8, 320), c.load_w_keys("wqa", 8, 384), c.load_w_keys("wqb", 3, 1536)
        K_BWO, K_WUV, K_WUKR = c.load_w_keys("bwo", 8, D), c.load_w_keys("wuv", 2, 1024), c.load_w_keys("wukr", 2, 1024)
        for rc in range(2):
            pb = ph.bankb(rc)
            for h in range(8):
                c.tr(pb[:, h * 128:(h + 1) * 128], wukr[:, rc, h * 128:(h + 1) * 128], ph.ident_b[:, :], r=K_WUKR + ["ident_b"], w=["bank%d" % rc])
            c.cp("dve", wukT[:, :, rc * 128:(rc + 1) * 128], pb[:, :].rearrange("p (h r) -> p h r", r=128), r=["bank%d" % rc], w=["wukT"])
        g_kv = ph.gain("g_kv", I["kv_norm"])
        g_kva = ph.gain("g_kva", I["kv_a_norm"])
        g_pre = ph.gain("g_pre", I["b_norm_pre"])
        g_post = ph.gain("g_post", I["b_norm_post"])
        g_qa = ph.gain("g_qa", I["b_q_a_norm"])
        negmask = ph.sb("negmask", [128, 128], F32)
        c.dma("sp", negmask[:], I["c_negmask"], w=["negmask"])
        cs = ph.sb("cs", [128, 64], F32)
        cf = ph.sb("cf", [128, 256], F32)
        cbf = ph.sb("cbf", [128, 256], BF16)
        kr = ph.sb("kr", [128, 64], F32)
        rt = ph.sb("rt", [128, 8, 32], F32)
        rt2 = ph.sb("rt2", [128, 8, 32], F32)
        kr2b = ph.sb("kr2b", [128, 128], BF16)
        qan = ph.sb("qan", [128, 384], BF16)
        qaT = ph.sb("qaT", [128, 3, 128], BF16)
        qnopeT = ph.sb("qnopeT", [128, 8, 128], BF16)
        qlatT = ph.sb("qlatT", [128, 16, 128], BF16)
        qpe = ph.sb("qpe", [128, 8, 64], F32)
        qpeb = ph.sb("qpeb", [128, 8, 64], BF16)
        ob = ph.sb("ob", [128, 8, 128], BF16)
        oT = ph.sb("oT", [128, 8, 128], BF16)
        olT = ph.sb("olT", [128, 2, 128], BF16)
        if not sample:
            cT_all = ph.sb("cT_all", [128, 2, LP], BF16)
            krT_all = ph.sb("krT_all", [128, LP], BF16)
            ctok_all = ph.sb("ctok_all", [128, 17, 256], BF16)
            qpeT = ph.sb("qpeT", [128, 4, 128], BF16)
            pbuf = [ph.sb("pbuf%d" % j, [128, LP], BF16) for j in range(2)]
            pT = [ph.sb("pT%d" % j, [128, 17, 128], BF16) for j in range(2)]
            sdg = [ph.sb("sdg%d" % j, [128, 128], F32) for j in range(2)]
            olT2 = [ph.sb("olT2_%d" % j, [128, 2, 128], BF16) for j in range(2)]
            sth = ph.sb("sth", [128, 2, 8], F32)
        else:
            cTn = ph.sb("cTn", [128, 2, 2], BF16)
            krTn = ph.sb("krTn", [128, 2], BF16)
            ptb = ph.sb("ptb", [128, NS * 8], I32)
            idx_all = ph.sb("idx_all", [128, NS * 8], I32)
            iota = ph.sb("iota", [128, 2], F32)
            c.dma("sp", iota[:], I["c_iota"], w=["iota"])
            for jl in range(8):
                c.dma("sp", ptb[16 * jl:16 * jl + 16, :], I["page_table"][jl:jl + 1, :].partition_broadcast(16), w=["ptb%d" % jl])
            c.ts("dve", idx_all[:], ptb[:], 16.0, iota[:, 1:2], ALU.mult, ALU.add, r=["ptb%d" % jl for jl in range(8)] + ["iota"], w=["idx"])
            cpg = ph.sb("cpg", [128, 8, 2048], BF16)
            rpg = ph.sb("rpg", [128, 8, 512], BF16)
            cTg = [ph.sb("cTg%d" % j, [128, 4, 2, 128], BF16) for j in range(2)]
            krTg = [ph.sb("krTg%d" % j, [64, 4, 128], BF16) for j in range(2)]
            s_sb = ph.sb("s_sb", [8, 8200], F32)
            p_sb = ph.sb("p_sb", [8, 8200], BF16)
            qls = ph.sb("qls", [128, 2, 8], BF16)
            qps = ph.sb("qps", [64, 8], BF16)
            pTs = ph.sb("pTs", [128, NPG, 8], BF16)
            pself = ph.sb("pself", [1, 8], BF16)
            olTs = ph.sb("olTs", [128, 2, 8], BF16)
            st8 = ph.sb("st8", [8, 8], F32)
            rlrow = ph.sb("rlrow", [1, 8], F32)
        flat_c = I["cache_kv_latent"].rearrange("(r t) c -> r (t c)", t=8)
        flat_r = I["cache_k_rope"].rearrange("(r t) c -> r (t c)", t=8)

        def cblk(j, rc):
            return cpg[:, j // 8, (j % 8) * 256 + rc * 128:(j % 8) * 256 + (rc + 1) * 128]

        def rblk(j):
            return rpg[:, j // 8, (j % 8) * 64:(j % 8 + 1) * 64]

        def rope(dst, src, n, H, rk, wk):
            cosb = cs[:n, 0:32].unsqueeze(1).to_broadcast([n, H, 32])
            sinb = cs[:n, 32:64].unsqueeze(1).to_broadcast([n, H, 32])
            x1, x2 = src[:, :, 0:32], src[:, :, 32:64]
            c.tt("dve", rt[:n, :H, :], x1, cosb, ALU.mult, r=rk + ["cs"], w=["rt"])
            c.tt("dve", rt2[:n, :H, :], x2, sinb, ALU.mult, r=rk + ["cs"], w=["rt2"])
            c.tt("dve", dst[:, :, 0:32], rt[:n, :H, :], rt2[:n, :H, :], ALU.subtract, r=["rt", "rt2"], w=wk)
            c.tt("dve", rt[:n, :H, :], x2, cosb, ALU.mult, r=rk + ["cs"], w=["rt"])
            c.tt("dve", rt2[:n, :H, :], x1, sinb, ALU.mult, r=rk + ["cs"], w=["rt2"])
            c.tt("dve", dst[:, :, 32:64], rt[:n, :H, :], rt2[:n, :H, :], ALU.add, r=["rt", "rt2"], w=wk)

        my_tiles = [t for t in tiles if (t.kind == "s") == sample]
        for ti, t in enumerate(my_tiles):
            n = t.n
            xs = ti % 2
            ph.load_x(3, t, xs)
            xk = "x%d" % xs
            x = ph.x[xs]
            rrow = 2064 if sample else t.pos
            c.dma("sp", cs[:n, :], I["c_rope"][rrow:rrow + n, :], w=["cs"])
            ph.norm_T(x[:n, :], xk, g_kv, "g_kv", n, 1)
            for k in range(8):
                c.mm(ph.bank(2)[:n, 0:320], ph.hT[:, k, :n], kvwa[:, k, :], start=(k == 0), stop=(k == 7), r=["hT"] + K_KVWA, w=["bank2"])
            ph.rstd(ph.bank(2)[:n, 0:256], n, 256, 2, ["bank2"])
            c.stt(cf[:n, :], ph.bank(2)[:n, 0:256], ph.st[:n, 2:3], g_kva[:n, :], ALU.mult, ALU.mult, r=["bank2", "st2", "g_kva"], w=["cf"])
            c.dma("sp", O["s_kv_latent"][t.s:t.s + 1, :] if sample else O["p_kv_latent"][t.b, t.pos:t.pos + n, :], cf[:n, :], r=["cf"], w=["kv_out"])
            cb = cbf[:n, :] if sample else ctok_all[:n, t.i, :]
            cbk = "cbf" if sample else "ctok_all"
            c.cp("act", cb, cf[:n, :], r=["cf"], w=[cbk])
            pb = ph.bankb(3)
            for rc in range(2):
                c.tr(pb[:, rc * 128:rc * 128 + n], cb[:, rc * 128:(rc + 1) * 128], ph.ident_b[:n, :n], r=[cbk, "ident_b"], w=["bank3"])
            rope(kr[:n, :].rearrange("p (h e) -> p h e", h=1), ph.bank(2)[:n, 256:320].rearrange("p (h e) -> p h e", h=1), n, 1, ["bank2"], ["kr"])
            c.dma("sp", O["s_k_rope"][t.s:t.s + 1, :] if sample else O["p_k_rope"][t.b, t.pos:t.pos + n, :], kr[:n, :], r=["kr"], w=["kr_out"])
            c.cp("act", kr2b[:n, 0:64], kr[:n, :], r=["kr"], w=["kr2b"])
            c.cp("act", kr2b[:n, 64:128], kr[:n, :], r=["kr"], w=["kr2b"])
            c.tr(pb[:, 256:256 + n], kr2b[:n, :], ph.ident_b[:n, :n], r=["kr2b", "ident_b"], w=["bank3"])
            if sample:
                c.cp("dve", cTn[:, :, 0:1], pb[:, 0:256].rearrange("p (r t) -> p r t", t=128)[:, :, 0:1], r=["bank3"], w=["cTn"])
                c.cp("dve", krTn[:, 0:1], pb[:, 256:257], r=["bank3"], w=["krTn"])
            else:
                c.cp("dve", cT_all[:, :, t.pos:t.pos + n], pb[:, 0:256].rearrange("p (r t) -> p r t", t=128)[:, :, :n], r=["bank3"], w=["cT_all"])
                c.cp("dve", krT_all[:, t.pos:t.pos + n], pb[:, 256:256 + n], r=["bank3"], w=["krT_all"])
            ph.norm_T(x[:n, :], xk, g_pre, "g_pre", n, 1)
            for k in range(8):
                c.mm(ph.bank(2)[:n, 0:384], ph.hT[:, k, :n], wqa[:, k, :], start=(k == 0), stop=(k == 7), r=["hT"] + K_WQA, w=["bank2"])
            ph.norm_T(ph.bank(2)[:n, 0:384], "bank2", g_qa, "g_qa", n, 4, width=384, hn=qan, hT=qaT, hkey="qaT", col=3)
            for h in range(8):
                bk = 5 + h // 4
                for k in range(3):
                    c.mm(ph.bank(bk)[:, (h % 4) * 128:(h % 4) * 128 + n], wqb[:, k, h * 192:h * 192 + 128], qaT[:, k, :n], start=(k == 0), stop=(k == 2), r=["qaT"] + K_WQB, w=["bank%d" % bk])
            for g in range(2):
                c.cp("act", qnopeT[:, g * 4:(g + 1) * 4, :n], ph.bank(5 + g).rearrange("p (j t) -> p j t", t=128)[:, :, :n], r=["bank%d" % (5 + g)], w=["qnopeT"])
            for h in range(8):
                for rc in range(2):
                    q = h * 2 + rc
                    bk = 2 + q // 4
                    c.mm(ph.bank(bk)[:, (q % 4) * 128:(q % 4) * 128 + n], wukT[:, h, rc * 128:(rc + 1) * 128], qnopeT[:, h, :n], r=["wukT", "qnopeT"], w=["bank%d" % bk])
            for g in range(4):
                c.act(qlatT[:, g * 4:(g + 1) * 4, :n], ph.bank(2 + g).rearrange("p (j t) -> p j t", t=128)[:, :, :n], AF.Copy, r=["bank%d" % (2 + g)], w=["qlatT"], scale=MLA_SCALE)
            for k in range(3):
                c.mm(ph.bank(6)[:n, :], qaT[:, k, :n], wqb[:, k, :].rearrange("p (h e) -> p h e", e=192)[:, :, 128:192], start=(k == 0), stop=(k == 2), r=["qaT"] + K_WQB, w=["bank6"])
            rope(qpe[:n], ph.bank(6)[:n, :].rearrange("p (h e) -> p h e", e=64), n, 8, ["bank6"], ["qpe"])
            c.act(qpeb[:n], qpe[:n], AF.Copy, r=["qpe"], w=["qpeb"], scale=MLA_SCALE)
            if not sample:
                pb7 = ph.bankb(7)
                for j in range(4):
                    c.tr(pb7[:, j * 128:j * 128 + n], qpeb[:n, 2 * j:2 * j + 2, :].rearrange("p h e -> p (h e)"), ph.ident_b[:n, :n], r=["qpeb", "ident_b"], w=["bank7"])
                c.cp("dve", qpeT[:, :, :n], pb7[:, 0:512].rearrange("p (j t) -> p j t", t=128)[:, :, :n], r=["bank7"], w=["qpeT"])
                K_ = t.pos + n
                nb = (K_ + 511) // 512
                nkt = t.i + 1
                piped = nb <= 4

                def region(h):
                    if piped:
                        rb = 4 * (h % 2)
                        return rb, rb, rb + 3
                    return 0, 5, 0

                def st_S(h):
                    sbk = region(h)[0]
                    hb = 64 * (h % 2)
                    for kb in range(nb):
                        k0, k1 = kb * 512, min(K_, kb * 512 + 512)
                        bk = sbk + kb
                        bkk = "bank%d" % bk
                        c.mm(ph.bank(bk)[:n, 0:k1 - k0], qlatT[:, 2 * h, :n], cT_all[:, 0, k0:k1], start=True, stop=False, r=["qlatT", "cT_all"], w=[bkk])
                        c.mm(ph.bank(bk)[:n, 0:k1 - k0], qlatT[:, 2 * h + 1, :n], cT_all[:, 1, k0:k1], start=False, stop=False, r=["qlatT", "cT_all"], w=[bkk])
                        c.mm(ph.bank(bk)[:n, 0:k1 - k0], qpeT[hb:hb + 64, h // 2, :n], krT_all[hb:hb + 64, k0:k1], start=False, stop=True, r=["qpeT", "krT_all"], w=[bkk])

                def st_X(h):
                    q = h % 2
                    sbk = region(h)[0]
                    sck = ["bank%d" % (sbk + kb) for kb in range(nb)]
                    sc = ph.ps[:n, sbk * 512:sbk * 512 + K_]
                    sk = lambda j: "sth%d_%d" % (q, j)
                    sv = lambda j: sth[:n, q, j:j + 1]
                    c.red(sv(0), sc, ALU.max, r=sck, w=[sk(0)])
                    c.ts("dve", sv(1), sv(0), -1.0, None, ALU.mult, r=[sk(0)], w=[sk(1)])
                    c.tt("dve", sdg[q][:n, :n], sc[:, t.pos:K_], negmask[:n, :n], ALU.add, r=sck + ["negmask"], w=["sdg%d" % q])
                    if t.pos > 0:
                        c.act(pbuf[q][:n, 0:t.pos], sc[:, 0:t.pos], AF.Exp, r=sck + [sk(1)], w=["pbuf%d" % q, sk(2)], bias=sv(1), accum_out=sv(2))
                    c.act(pbuf[q][:n, t.pos:K_], sdg[q][:n, :n], AF.Exp, r=["sdg%d" % q, sk(1)], w=["pbuf%d" % q, sk(3)], bias=sv(1), accum_out=sv(3))
                    if t.pos > 0:
                        c.tt("dve", sv(3), sv(3), sv(2), ALU.add, r=[sk(2), sk(3)], w=[sk(3)])
                    c.recip(sv(4), sv(3), r=[sk(3)], w=[sk(4)])

                def st_T(h):
                    q = h % 2
                    tbk = region(h)[1]
                    for kt in range(nkt):
                        k0 = 0 if kt == 0 else 16 + 128 * (kt - 1)
                        nk = 16 if kt == 0 else 128
                        bk = tbk + kt // 8
                        c.tr(ph.bankb(bk)[:nk, (kt % 8) * 128:(kt % 8) * 128 + n], pbuf[q][:n, k0:k0 + nk], ph.ident_b[:n, :n], r=["pbuf%d" % q, "ident_b"], w=["bank%d" % bk])
                    for j in range((nkt + 7) // 8):
                        m = min(8, nkt - j * 8)
                        c.cp("act" if j % 2 == 0 else "dve", pT[q][:, j * 8:j * 8 + m, :n], ph.bankb(tbk + j)[:, 0:m * 128].rearrange("p (j t) -> p j t", t=128)[:, :, :n], r=["bank%d" % (tbk + j)], w=["pT%d" % q])

                def st_V(h):
                    q = h % 2
                    vbk = region(h)[2]
                    vk = "bank%d" % vbk
                    for rc in range(2):
                        for kt in range(nkt):
                            nk = 16 if kt == 0 else 128
                            c.mm(ph.bank(vbk)[:, rc * 128:rc * 128 + n], ctok_all[:nk, kt, rc * 128:(rc + 1) * 128], pT[q][:nk, kt, :n], start=(kt == 0), stop=(kt == nkt - 1), r=["ctok_all", "pT%d" % q], w=[vk])
                    c.cp("dve", olT2[q][:, :, :n], ph.bank(vbk)[:, 0:256].rearrange("p (r t) -> p r t", t=128)[:, :, :n], r=[vk], w=["olT%d" % q])
                    for rc in range(2):
                        c.mm(ph.bank(vbk)[:n, 256:384], olT2[q][:, rc, :n], wuv[:, rc, h * 128:(h + 1) * 128], start=(rc == 0), stop=(rc == 1), r=["olT%d" % q] + K_WUV, w=[vk])
                    c.act(ob[:n, h, :], ph.bank(vbk)[:n, 256:384], AF.Identity, r=[vk, "sth%d_4" % q], w=["ob"], scale=sth[:n, q, 4:5])

                if piped:
                    st_S(0)
                    for h in range(8):
                        if h < 7:
                            st_S(h + 1)
                        st_X(h)
                        st_T(h)
                        st_V(h)
                else:
                    for h in range(8):
                        st_S(h)
                        st_X(h)
                        st_T(h)
                        st_V(h)
            else:
                c.cp("dve", qls[:, :, :], qlatT[:, :, 0:1].rearrange("p (h r) t -> p r (h t)", r=2), r=["qlatT"], w=["qls"])
                pb7 = ph.bankb(7)
                for h in range(8):
                    c.tr(pb7[:64, 2 * h:2 * h + 1], qpeb[:1, h, :], ph.ident_b[:1, :1], r=["qpeb", "ident_b"], w=["bank7"])
                c.cp("dve", qps[:, :], pb7[:64, 0:16].rearrange("p (h two) -> p h two", two=2)[:, :, 0], r=["bank7"], w=["qps"])
                for m in range(8):
                    icol = idx_all[:, t.s * 8 + m:t.s * 8 + m + 1]
                    c.P.op("pool", lambda e, m=m, icol=icol: e.indirect_dma_start(out=cpg[:, m, :], out_offset=None, in_=flat_c, in_offset=bass.IndirectOffsetOnAxis(ap=icol, axis=0)), ["idx"], ["cpg%d" % m], dma=True)
                    c.P.op("pool", lambda e, m=m, icol=icol: e.indirect_dma_start(out=rpg[:, m, :], out_offset=None, in_=flat_r, in_offset=bass.IndirectOffsetOnAxis(ap=icol, axis=0)), ["idx"], ["rpg%d" % m], dma=True)
                for g in range(16):
                    A, B, C = g % 2, 2 + g % 2, 4 + g % 2
                    pa, pbb = ph.bankb(A), ph.bankb(B)
                    for j in range(4):
                        for rc in range(2):
                            c.tr(pa[:, (j * 2 + rc) * 128:(j * 2 + rc + 1) * 128], cblk(4 * g + j, rc), ph.ident_b[:, :], r=["cpg%d" % (g // 2), "ident_b"], w=["bank%d" % A])
                        c.tr(pbb[:64, j * 128:(j + 1) * 128], rblk(4 * g + j), ph.ident_b[:, :], r=["rpg%d" % (g // 2), "ident_b"], w=["bank%d" % B])
                    c.cp("act", cTg[g % 2][:], pa[:, :].rearrange("p (j r t) -> p j r t", r=2, t=128), r=["bank%d" % A], w=["cTg%d" % (g % 2)])
                    c.cp("dve", krTg[g % 2][:], pbb[:64, 0:512].rearrange("p (j t) -> p j t", t=128), r=["bank%d" % B], w=["krTg%d" % (g % 2)])
                    for rc in range(2):
                        c.mm(ph.bank(C)[:8, :], qls[:, rc, :], cTg[g % 2][:, :, rc, :], start=(rc == 0), stop=False, r=["qls", "cTg%d" % (g % 2)], w=["bank%d" % C])
                    c.mm(ph.bank(C)[:8, :], qps[:64, :], krTg[g % 2][:64, :, :], start=False, stop=True, r=["qps", "krTg%d" % (g % 2)], w=["bank%d" % C])
                    c.cp("act", s_sb[:8, g * 512:(g + 1) * 512], ph.bank(C)[:8, :], r=["bank%d" % C], w=["s_sb"])
                for rc in range(2):
                    c.mm(ph.bank(6)[:8, 0:1], qls[:, rc, :], cTn[:, rc, 0:1], start=(rc == 0), stop=False, r=["qls", "cTn"], w=["bank6"])
                c.mm(ph.bank(6)[:8, 0:1], qps[:64, :], krTn[:64, 0:1], start=False, stop=True, r=["qps", "krTn"], w=["bank6"])
                c.cp("act", s_sb[:8, 8192:8193], ph.bank(6)[:8, 0:1], r=["bank6"], w=["s_sb"])
                c.red(st8[:8, 0:1], s_sb[:8, 0:8193], ALU.max, r=["s_sb"], w=["st8a"])
                c.ts("dve", st8[:8, 1:2], st8[:8, 0:1], -1.0, None, ALU.mult, r=["st8a"], w=["st8b"])
                c.act(p_sb[:8, 0:8193], s_sb[:8, 0:8193], AF.Exp, r=["s_sb", "st8b"], w=["p_sb", "st8c"], bias=st8[:8, 1:2], accum_out=st8[:8, 2:3])
                c.recip(st8[:8, 3:4], st8[:8, 2:3], r=["st8c"], w=["st8d"])
                pb6 = ph.bankb(6)
                for j in range(NPG):
                    c.tr(pb6[:, j * 8:j * 8 + 8], p_sb[:8, j * 128:(j + 1) * 128], ph.ident_b[:8, :8], r=["p_sb", "ident_b"], w=["bank6"])
                c.cp("act", pTs[:], pb6[:, 0:512].rearrange("p (j h) -> p j h", h=8), r=["bank6"], w=["pTs"])
                c.tr(pb7[:1, 0:8], p_sb[:8, 8192:8193], ph.ident_b[:8, :8], r=["p_sb", "ident_b"], w=["bank7"])
                c.cp("dve", pself[:1, :], pb7[:1, 0:8], r=["bank7"], w=["pself"])
                for rc in range(2):
                    for j in range(NPG):
                        c.mm(ph.bank(0)[:, rc * 8:rc * 8 + 8], cblk(j, rc), pTs[:, j, :], start=(j == 0), stop=False, r=["cpg%d" % (j // 8), "pTs"], w=["bank0"])
                    c.mm(ph.bank(0)[:, rc * 8:rc * 8 + 8], cbf[0:1, rc * 128:(rc + 1) * 128], pself[0:1, :], start=False, stop=True, r=["cbf", "pself"], w=["bank0"])
                c.cp("dve", olTs[:], ph.bank(0)[:, 0:16].rearrange("p (r h) -> p r h", h=8), r=["bank0"], w=["olTs"])
                for h in range(8):
                    for rc in range(2):
                        c.mm(ph.bank(1 + h // 4)[:1, (h % 4) * 128:(h % 4 + 1) * 128], olTs[:, rc, h:h + 1], wuv[:, rc, h * 128:(h + 1) * 128], start=(rc == 0), stop=(rc == 1), r=["olTs"] + K_WUV, w=["bank%d" % (1 + h // 4)])
                c.tr(ph.bank(3)[:1, 0:8], st8[:8, 3:4], ph.ident_f[:8, :8], r=["st8d", "ident_f"], w=["bank3"])
                c.cp("dve", rlrow[:1, :], ph.bank(3)[:1, 0:8], r=["bank3"], w=["rlrow"])
                c.tt("dve", ob[:1], ph.bank(1, 2)[:1, :].rearrange("p (h d) -> p h d", d=128), rlrow[:1, :].unsqueeze(2).to_broadcast([1, 8, 128]), ALU.mult, r=["bank1", "bank2", "rlrow"], w=["ob"])
            pb5 = ph.bankb(5)
            for h in range(8):
                c.tr(pb5[:, h * 128:h * 128 + n], ob[:n, h, :], ph.ident_b[:n, :n], r=["ob", "ident_b"], w=["bank5"])
            c.cp("act", oT[:, :, :n], pb5[:, :].rearrange("p (k t) -> p k t", t=128)[:, :, :n], r=["bank5"], w=["oT"])
            for hf in range(2):
                for k in range(8):
                    c.mm(ph.bank(hf)[:n, :], oT[:, k, :n], bwo[:, k, hf * 512:(hf + 1) * 512], start=(k == 0), stop=(k == 7), r=["oT"] + K_BWO, w=["bank%d" % hf])
            ph.post_norm_res(0, n, g_post, "g_post", xs)
            ph.store_x(3, t, xs)
        ph.finish()

    PH = {1: phase_gdn, 2: lambda: phase_ffn(0), 3: lambda: (phase_mla(False), phase_mla(True)), 31: lambda: phase_mla(False), 32: lambda: phase_mla(True), 4: lambda: phase_ffn(1)}
    for p_ in phases:
        PH[p_]()
    ges.close()
    return nc


def _consts():
    i = np.arange(128)
    c = {}
    c["c_ident"] = np.eye(128, dtype=np.float32)
    c["c_triu"] = (i[:, None] <= i[None, :]).astype(np.float32)
    c["c_lstrict"] = (i[:, None] > i[None, :]).astype(np.float32)
    c["c_lincl"] = (i[:, None] >= i[None, :]).astype(np.float32)
    c["c_negmask"] = np.where(i[None, :] <= i[:, None], 0.0, NEG).astype(np.float32)
    pos = np.concatenate([np.arange(LP), [8192]]).astype(np.float32)
    inv = (10000.0 ** (-np.arange(32, dtype=np.float32) / 32)).astype(np.float32)
    ang = (pos[:, None] * inv[None, :]).astype(np.float32)
    c["c_rope"] = np.concatenate([np.cos(ang), np.sin(ang)], axis=1).astype(np.float32)
    c["c_iota"] = np.stack([i, i % 16], axis=1).astype(np.float32)
    return c


def make_in_maps(inp):
    f = lambda a: np.ascontiguousarray(a)
    shared = {
        "cache_kv_latent": f(inp["cache_kv_latent"]).reshape(-1, 256),
        "cache_k_rope": f(inp["cache_k_rope"]).reshape(-1, 64),
        "meta_tokens": f(inp["meta_tokens"]),
        "a_norm_pre": f(inp["a_norm_pre"]), "a_norm_post": f(inp["a_norm_post"]), "a_w_in": f(inp["a_w_in"][0]),
        "a_conv_w": f(inp["a_conv_w"][0]), "a_log": f(inp["a_log"]), "a_dt_bias": f(inp["a_dt_bias"]),
        "a_out_norm": f(inp["a_out_norm"]), "a_w_out": f(inp["a_w_out"][0]),
        "kv_norm": f(inp["kv_norm"]).reshape(1, -1), "kv_w_a": f(inp["kv_w_a"]), "kv_a_norm": f(inp["kv_a_norm"]).reshape(1, -1),
        "kv_w_uk": f(inp["kv_w_uk"]).reshape(256, 1024), "kv_w_uv": f(inp["kv_w_uv"]).reshape(256, 1024),
        "b_norm_pre": f(inp["b_norm_pre"]), "b_norm_post": f(inp["b_norm_post"]), "b_w_q_a": f(inp["b_w_q_a"][0]),
        "b_q_a_norm": f(inp["b_q_a_norm"]), "b_w_q_b": f(inp["b_w_q_b"][0]), "b_w_out": f(inp["b_w_out"][0]),
        "f_norm_pre": f(inp["f_norm_pre"]), "f_norm_post": f(inp["f_norm_post"]), "f_w_up": f(inp["f_w_up"]),
        "f_conv_w": f(inp["f_conv_w"]), "f_conv_b": f(inp["f_conv_b"]), "f_w_down": f(inp["f_w_down"]),
    }
    shared.update(_consts())
    maps = []
    for cidx in range(8):
        m = dict(shared)
        m["x_prompt"] = f(inp["x_prompt"][NB * cidx:NB * (cidx + 1)])
        sl = slice(NS * cidx, NS * (cidx + 1))
        m["x_sample"] = f(inp["x_sample"][sl, 0])
        m["state_delta_S"] = f(inp["state_delta_S"][0, sl])
        m["state_delta_conv"] = f(inp["state_delta_conv"][0, sl])
        m["state_ffn_conv"] = f(inp["state_ffn_conv"][:, sl])
        m["page_table"] = f(inp["page_table"][sl].astype(np.int32).reshape(NS, 8, 8).transpose(2, 0, 1).reshape(8, NS * 8))
        maps.append(m)
    return maps


def assemble(res):
    cat = lambda k, ax=0: np.concatenate([r[k] for r in res], axis=ax)
    return (
        cat("y_prompt"), cat("y_sample")[:, None, :], cat("p_delta_S")[None], cat("p_delta_conv")[None],
        cat("p_ffn_conv", 1), cat("p_kv_latent"), cat("p_k_rope"), cat("s_delta_S")[None], cat("s_delta_conv")[None],
        cat("s_ffn_conv", 1), cat("s_kv_latent")[:, None, :], cat("s_k_rope")[:, None, :],
    )


def kernel(**inputs):
    inp = {k: np.asarray(v) for k, v in inputs.items()}
    nc = build_program()
    res = run_bass_kernel_spmd(nc, make_in_maps(inp), core_ids=list(range(8)))
    return tuple(np.ascontiguousarray(a.astype(np.float32)) for a in assemble(res.results))
```
